# Optimizing a Trainium2 kernel written in Bass

```python
import jax, jax.numpy as jnp
from jax import lax
import numpy as np

D_MODEL = 1024
BATCH = 8
SEQ = 4096
DEPTH = 2

HEAD_DIM = 64
ATTN_Q_HEADS = 8
ATTN_KV_HEADS = 2
WINDOW = 128
SGU_GROUPS = 4
SGU_GROUP_DIM = 64
SGU_CHUNK = 128
GLA_HEADS = 4
GLA_KEY_DIM = 64
GLA_VALUE_DIM = 64
GLA_GATE_RANK = 16
GLA_TAU = 16.0
GLA_CHUNK = 64
D_FF = 2816
RMS_EPS = 1e-6
NEG_INF = -1e30

ATTN_Q_W = ATTN_Q_HEADS * HEAD_DIM
ATTN_KV_W = ATTN_KV_HEADS * HEAD_DIM
SGU_W = SGU_GROUPS * SGU_GROUP_DIM
GLA_QK_W = GLA_HEADS * GLA_KEY_DIM
GLA_V_W = GLA_HEADS * GLA_VALUE_DIM
MIX_W = ATTN_Q_W + SGU_W + GLA_V_W
IN_SPLITS = (ATTN_Q_W, ATTN_KV_W, ATTN_KV_W, SGU_W, SGU_W,
             GLA_QK_W, GLA_QK_W, GLA_V_W, GLA_V_W, GLA_GATE_RANK)
IN_W = ATTN_Q_W + 2 * ATTN_KV_W + 2 * SGU_W + 2 * GLA_QK_W + 2 * GLA_V_W + GLA_GATE_RANK

kernel_name = "hymba_style_swa_sgu_gla_macaron"


def rms_norm(x, g):
    xf = x.astype(jnp.float32)
    y = xf * lax.rsqrt(jnp.mean(xf * xf, axis=-1, keepdims=True) + RMS_EPS)
    return (y * g.astype(jnp.float32)).astype(x.dtype)


def swiglu_ffn(h, w_gate, w_up, w_down):
    return (jax.nn.silu(h @ w_gate) * (h @ w_up)) @ w_down


def split_columns(p):
    outs, start = [], 0
    for size in IN_SPLITS:
        outs.append(p[..., start:start + size])
        start += size
    return outs


def sliding_window_sink_attention(q, k, v, sinks):
    B, T, H, Dh = q.shape
    Hkv = k.shape[2]
    G = H // Hkv
    nb = T // WINDOW
    qb = q.astype(jnp.float32).reshape(B, nb, WINDOW, Hkv, G, Dh)

    def banded(t):
        cur = t.astype(jnp.float32).reshape(B, nb, WINDOW, Hkv, Dh)
        prev = jnp.pad(cur, ((0, 0), (1, 0), (0, 0), (0, 0), (0, 0)))[:, :-1]
        return jnp.concatenate([prev, cur], axis=2)

    kb, vb = banded(k), banded(v)
    scores = jnp.einsum('bnqhgd,bnkhd->bnhgqk', qb, kb) * (Dh ** -0.5)
    qi = jnp.arange(WINDOW)[:, None]
    kj = jnp.arange(2 * WINDOW)[None, :]
    rel = qi + WINDOW - kj
    band = (rel >= 0) & (rel < WINDOW)
    blk = jnp.arange(nb)[:, None, None]
    mask = band[None] & ((blk > 0) | (kj[None] >= WINDOW))
    scores = jnp.where(mask[None, :, None, None], scores, NEG_INF)
    sink = jnp.broadcast_to(sinks.astype(jnp.float32).reshape(1, 1, Hkv, G, 1, 1),
                            scores.shape[:-1] + (1,))
    probs = jax.nn.softmax(jnp.concatenate([scores, sink], axis=-1), axis=-1)[..., :-1]
    out = jnp.einsum('bnhgqk,bnkhd->bnqhgd', probs, vb)
    return out.reshape(B, T, H * Dh).astype(v.dtype)


def chunked_spatial_gating(u, v, v_norm, w_s, b_s):
    B, T, _ = u.shape
    nc = T // SGU_CHUNK
    vg = v.reshape(B, T, SGU_GROUPS, SGU_GROUP_DIM)
    vn = rms_norm(vg, v_norm.reshape(SGU_GROUPS, SGU_GROUP_DIM))
    vc = vn.reshape(B, nc, SGU_CHUNK, SGU_GROUPS, SGU_GROUP_DIM).astype(jnp.float32)
    w_causal = jnp.tril(w_s.astype(jnp.float32))
    s = jnp.einsum('gts,bnsgc->bntgc', w_causal, vc) + b_s.astype(jnp.float32).T[None, None, :, :, None]
    return (u.astype(jnp.float32) * s.reshape(B, T, SGU_W)).astype(u.dtype)


def gated_linear_attention(q, k, v, g_log, gate, out_norm):
    B, T, H, dk = q.shape
    dv = v.shape[-1]
    N = T // GLA_CHUNK
    C = GLA_CHUNK

    def chunk(t):
        return t.astype(jnp.float32).reshape(B, N, C, H, t.shape[-1]).transpose(0, 3, 1, 2, 4)

    qc = chunk(q) * (dk ** -0.5)
    kc, vc, gc = chunk(k), chunk(v), chunk(g_log)
    b = jnp.cumsum(gc, axis=-2)
    b_last = b[..., -1:, :]
    q_dec = qc * jnp.exp(b)
    k_intra = kc * jnp.exp(-b)
    k_state = kc * jnp.exp(b_last - b)
    causal = jnp.tril(jnp.ones((C, C), dtype=bool))
    attn = jnp.where(causal, jnp.einsum('bhncd,bhnsd->bhncs', q_dec, k_intra), 0.0)
    o_intra = jnp.einsum('bhncs,bhnse->bhnce', attn, vc)
    delta = jnp.einsum('bhncd,bhnce->bhnde', k_state, vc)
    decay = jnp.exp(b_last[..., 0, :])

    def step(S, inp):
        dec, dlt = inp
        return dec[..., None] * S + dlt, S

    S0 = jnp.zeros((B, H, dk, dv), jnp.float32)
    _, S_prev = lax.scan(step, S0, (decay.transpose(2, 0, 1, 3), delta.transpose(2, 0, 1, 3, 4)))
    S_prev = S_prev.transpose(1, 2, 0, 3, 4)
    o_inter = jnp.einsum('bhncd,bhnde->bhnce', q_dec, S_prev)
    o = (o_intra + o_inter).transpose(0, 2, 3, 1, 4).reshape(B, T, H, dv)
    o = rms_norm(o, out_norm) * jax.nn.silu(gate.astype(jnp.float32))
    return o.reshape(B, T, H * dv).astype(v.dtype)


def setup_inputs(seed: int = 0) -> dict:
    key = jax.random.key(seed)
    ks = jax.random.split(key, 24)
    f32 = jnp.float32

    def nrm(k, shape, scale):
        return jax.random.normal(k, shape, f32) * scale

    def gain(k, shape):
        return 1.0 + 0.02 * jax.random.normal(k, shape, f32)

    L, D, F = DEPTH, D_MODEL, D_FF
    return {
        "x": nrm(ks[0], (BATCH, SEQ, D), 1.0),
        "ffn1_norm": gain(ks[1], (L, D)),
        "ffn1_w_gate": nrm(ks[2], (L, D, F), D ** -0.5),
        "ffn1_w_up": nrm(ks[3], (L, D, F), D ** -0.5),
        "ffn1_w_down": nrm(ks[4], (L, F, D), F ** -0.5),
        "mix_norm": gain(ks[5], (L, D)),
        "w_in": nrm(ks[6], (L, D, IN_W), D ** -0.5),
        "attn_q_norm": gain(ks[7], (L, HEAD_DIM)),
        "attn_k_norm": gain(ks[8], (L, HEAD_DIM)),
        "attn_sinks": nrm(ks[9], (L, ATTN_Q_HEADS), 1.0),
        "sgu_v_norm": gain(ks[10], (L, SGU_W)),
        "sgu_w": nrm(ks[11], (L, SGU_GROUPS, SGU_CHUNK, SGU_CHUNK), SGU_CHUNK ** -0.5),
        "sgu_b": 1.0 + 0.1 * jax.random.normal(ks[12], (L, SGU_GROUPS, SGU_CHUNK), f32),
        "gla_w_gate_up": nrm(ks[13], (L, GLA_GATE_RANK, GLA_QK_W), GLA_GATE_RANK ** -0.5),
        "gla_b_gate": nrm(ks[14], (L, GLA_QK_W), 0.1),
        "gla_out_norm": gain(ks[15], (L, GLA_VALUE_DIM)),
        "w_out": nrm(ks[16], (L, MIX_W, D), MIX_W ** -0.5),
        "ffn2_norm": gain(ks[17], (L, D)),
        "ffn2_w_gate": nrm(ks[18], (L, D, F), D ** -0.5),
        "ffn2_w_up": nrm(ks[19], (L, D, F), D ** -0.5),
        "ffn2_w_down": nrm(ks[20], (L, F, D), F ** -0.5),
    }


def reference(x, ffn1_norm, ffn1_w_gate, ffn1_w_up, ffn1_w_down, mix_norm, w_in,
              attn_q_norm, attn_k_norm, attn_sinks, sgu_v_norm, sgu_w, sgu_b,
              gla_w_gate_up, gla_b_gate, gla_out_norm, w_out,
              ffn2_norm, ffn2_w_gate, ffn2_w_up, ffn2_w_down):
    B, T, _ = x.shape
    for l in range(DEPTH):
        x = x + 0.5 * swiglu_ffn(rms_norm(x, ffn1_norm[l]), ffn1_w_gate[l], ffn1_w_up[l], ffn1_w_down[l])

        h = rms_norm(x, mix_norm[l])
        p = h @ w_in[l]
        a_q, a_k, a_v, s_u, s_v, c_q, c_k, c_v, c_g, c_lr = split_columns(p)

        a_q = rms_norm(a_q.reshape(B, T, ATTN_Q_HEADS, HEAD_DIM), attn_q_norm[l])
        a_k = rms_norm(a_k.reshape(B, T, ATTN_KV_HEADS, HEAD_DIM), attn_k_norm[l])
        a_v = a_v.reshape(B, T, ATTN_KV_HEADS, HEAD_DIM)
        out_a = sliding_window_sink_attention(a_q, a_k, a_v, attn_sinks[l])

        out_b = chunked_spatial_gating(jax.nn.gelu(s_u), jax.nn.gelu(s_v),
                                       sgu_v_norm[l], sgu_w[l], sgu_b[l])

        gate_logits = (c_lr @ gla_w_gate_up[l] + gla_b_gate[l]).astype(jnp.float32)
        g_log = jax.nn.log_sigmoid(gate_logits) / GLA_TAU
        out_c = gated_linear_attention(
            c_q.reshape(B, T, GLA_HEADS, GLA_KEY_DIM),
            c_k.reshape(B, T, GLA_HEADS, GLA_KEY_DIM),
            c_v.reshape(B, T, GLA_HEADS, GLA_VALUE_DIM),
            g_log.reshape(B, T, GLA_HEADS, GLA_KEY_DIM),
            c_g.reshape(B, T, GLA_HEADS, GLA_VALUE_DIM),
            gla_out_norm[l])

        mixed = jnp.concatenate([out_a, out_b.astype(out_a.dtype), out_c.astype(out_a.dtype)], axis=-1)
        x = x + (mixed @ w_out[l]).astype(x.dtype)

        x = x + 0.5 * swiglu_ffn(rms_norm(x, ffn2_norm[l]), ffn2_w_gate[l], ffn2_w_up[l], ffn2_w_down[l])
    return x
```

```python
import numpy as np
from contextlib import ExitStack
import concourse.bass as bass
import concourse.mybir as mybir
from concourse.bass_utils import run_bass_kernel_spmd

F32 = mybir.dt.float32
BF16 = mybir.dt.bfloat16
AF = mybir.ActivationFunctionType
ALU = mybir.AluOpType
AX = mybir.AxisListType

D_MODEL = 1024
DEPTH = 2
D_FF = 2816
KC = D_MODEL // 128
FC = D_FF // 128
FH = FC // 2
RMS_EPS = 1e-6
UW = 2048
W1 = 2 * KC * 128
W2 = FH * 128
N_FM = 14
N_FMU = N_FM // 2
N_TMU = 4
N_OUTU = 4
FFN_COLS = 2 * (FH * W1 + KC * W2)
MIX_COLS = (N_FMU + N_TMU + N_OUTU) * UW
LAYER_COLS = 2 * FFN_COLS + MIX_COLS
NEG = -30000.0
import os as _os
_PCS = [int(v) for v in _os.environ.get('PCS', '0,1,2,3,4,5,6').split(',')]


class Buf:
    __slots__ = ("name", "t", "last_w", "readers", "dma_sem", "dma_cnt", "free")

    def __init__(self, name, t):
        self.name = name
        self.t = t
        self.last_w = None
        self.readers = {}
        self.dma_sem = None
        self.dma_cnt = 0
        self.free = True

    def __getitem__(self, idx):
        return self.t[idx]


class Ring:
    def __init__(self, bufs):
        self.bufs = bufs
        self.i = 0

    def get(self):
        b = self.bufs[self.i % len(self.bufs)]
        assert b.free, f"ring buffer {b.name} still live"
        b.free = False
        self.i += 1
        return b

    @staticmethod
    def rel(*bs):
        for b in bs:
            b.free = True


class Prog:
    ENGS = ("pe", "act", "dve", "pool", "sp")

    def __init__(self, nc, stack):
        self.nc = nc
        self.stack = stack
        self.items = {e: [] for e in self.ENGS}
        self.count = {e: 0 for e in self.ENGS}
        self.pending = {e: False for e in self.ENGS}
        self.seen = {e: {} for e in self.ENGS}
        self.hist = {}
        self.sems = {}
        for e in self.ENGS:
            self.sems[e] = stack.enter_context(nc.semaphore("s_" + e))

    def sbuf(self, name, shape, dtype):
        return Buf(name, self.stack.enter_context(self.nc.sbuf_tensor(name, list(shape), dtype)))

    def psum(self, name, shape, dtype=F32):
        return Buf(name, self.stack.enter_context(self.nc.psum_tensor(name, list(shape), dtype)))

    def _dma_sem(self, b):
        if b.dma_sem is None:
            key = "d_" + b.name
            self.sems[key] = self.stack.enter_context(self.nc.semaphore(key))
            b.dma_sem = key
        return b.dma_sem

    def _waits(self, eng, reads, writes, skip_self):
        deps = []
        for b in reads:
            if b.last_w is not None:
                deps.append(b.last_w)
        for b in writes:
            if b.last_w is not None:
                deps.append(b.last_w)
            deps.extend(b.readers.values())
        seen = self.seen[eng]
        best = {}
        for k, v in deps:
            if skip_self and k == eng:
                continue
            if seen.get(k, 0) >= v:
                continue
            if best.get(k, 0) < v:
                best[k] = v
        kept = []
        for k, v in sorted(best.items(), key=lambda kv: -len(self.hist.get(kv, ()))):
            if seen.get(k, 0) >= v:
                continue
            kept.append((k, v))
            seen[k] = v
            for k2, v2 in self.hist.get((k, v), {}).items():
                if seen.get(k2, 0) < v2:
                    seen[k2] = v2
        return kept

    def op(self, eng, name, *args, reads=(), writes=(), inc=True, skip_self=False, **kw):
        fn = (name, args, kw)
        waits = self._waits(eng, reads, writes, skip_self)
        if inc:
            self.count[eng] += 1
            ev = (eng, self.count[eng])
            self.pending[eng] = False
            self.hist[ev] = dict(self.seen[eng])
        else:
            ev = (eng, self.count[eng] + 1)
            self.pending[eng] = True
        self.items[eng].append((waits, fn, (eng, 1) if inc else None))
        for b in writes:
            b.last_w = ev
            b.readers = {}
        for b in reads:
            if b not in writes:
                b.readers[eng] = ev
        return ev

    def dma(self, out, in_, reads=(), writes=(), sem_buf=None, queue="sp"):
        fn = ("dma_start", (), dict(out=out, in_=in_))
        sb = sem_buf or (writes[0] if writes else reads[0])
        key = self._dma_sem(sb)
        waits = self._waits(queue, reads, writes, False)
        sb.dma_cnt += 16
        ev = (key, sb.dma_cnt)
        self.hist[ev] = dict(self.seen[queue])
        self.items[queue].append((waits, fn, (key, 16)))
        for b in writes:
            b.last_w = ev
            b.readers = {}
        for b in reads:
            if b not in writes:
                b.readers["q_" + key] = ev
        return ev

    def wait_event(self, eng, ev):
        k, v = ev
        if self.seen[eng].get(k, 0) < v:
            self.seen[eng][k] = v
            self.items[eng].append(([(k, v)], None, None))

    def emit(self):
        nc = self.nc
        for e in self.ENGS:
            assert not self.pending[e], f"engine {e} has pending un-inc'd instrs"
        with nc.Block() as block:
            def run(engname):
                def body(engine):
                    for waits, fn, inc in self.items[engname]:
                        fuse = (fn is not None and waits and "accum_out" not in fn[2])
                        for k, v in (waits[:-1] if fuse else waits):
                            engine.wait_ge(self.sems[k], v)
                        if fn is None:
                            continue
                        r = getattr(engine, fn[0])(*fn[1], **fn[2])
                        if fuse:
                            r = r._wait_ge(self.sems[waits[-1][0]], waits[-1][1])
                        if inc is not None:
                            r.then_inc(self.sems[inc[0]], inc[1])
                return body
            block.sync(run("sp"))
            block.tensor(run("pe"))
            block.scalar(run("act"))
            block.vector(run("dve"))
            block.gpsimd(run("pool"))


def _ffn_units(wg, wu, wd):
    g = wg.reshape(KC, 128, FC, 128)
    u = wu.reshape(KC, 128, FC, 128)
    d = wd.reshape(FC, 128, KC, 128)
    parts = []
    for hf in range(2):
        for f in range(FH):
            fg = hf * FH + f
            gu = np.stack([g[:, :, fg, :], u[:, :, fg, :]], axis=0)
            parts.append(gu.transpose(2, 0, 1, 3).reshape(128, W1))
        for dc in range(KC):
            blk = d[hf * FH:(hf + 1) * FH, :, dc, :]
            parts.append(blk.transpose(1, 0, 2).reshape(128, W2))
    return np.concatenate(parts, axis=1)


def _fm_unit(ca, cb):
    a = np.stack([ca.reshape(KC, 128, 128), cb.reshape(KC, 128, 128)], axis=0)
    return a.transpose(2, 0, 1, 3).reshape(128, UW)


def _tm_unit(pieces4, half):
    a = np.concatenate(pieces4, axis=1).reshape(KC, 128, 512)[4 * half:4 * half + 4]
    return a.transpose(1, 0, 2).reshape(128, UW)


def _mixer_units(w_in, w_out):
    z = np.zeros((D_MODEL, 128), np.float32)
    aq, ak, av = w_in[:, 0:512], w_in[:, 512:640], w_in[:, 640:768]
    su, sv = w_in[:, 768:1024], w_in[:, 1024:1280]
    cq, ck, cv, cg = w_in[:, 1280:1536], w_in[:, 1536:1792], w_in[:, 1792:2048], w_in[:, 2048:2304]
    clr = z.copy()
    clr[:, 0:16] = w_in[:, 2304:2320]
    qc = [np.concatenate([aq[:, c * 64:(c + 1) * 64], aq[:, (4 + c) * 64:(5 + c) * 64]], axis=1) for c in range(4)]
    fm = [clr, ak, qc[0], qc[1], qc[2], qc[3], su[:, 0:128], su[:, 128:256],
          cq[:, 0:128], cq[:, 128:256], ck[:, 0:128], ck[:, 128:256], cg[:, 0:128], cg[:, 128:256]]
    tm = [av, sv[:, 0:128], sv[:, 128:256], ck[:, 0:128], ck[:, 128:256], cv[:, 0:128], cv[:, 128:256], z]
    perm = []
    for c in range(4):
        perm += list(range(c * 64, (c + 1) * 64)) + list(range((4 + c) * 64, (5 + c) * 64))
    perm += list(range(512, 1024))
    wo = w_out[np.array(perm), :]
    parts = [_fm_unit(fm[2 * u], fm[2 * u + 1]) for u in range(N_FMU)]
    parts += [_tm_unit(tm[4 * (u // 2):4 * (u // 2) + 4], u % 2) for u in range(N_TMU)]
    parts += [_fm_unit(wo[:, (2 * u) * 128:(2 * u + 1) * 128], wo[:, (2 * u + 1) * 128:(2 * u + 2) * 128]) for u in range(N_OUTU)]
    return np.concatenate(parts, axis=1)


def _pack_layer_weights(inp, l):
    cols = [_ffn_units(inp["ffn1_w_gate"][l], inp["ffn1_w_up"][l], inp["ffn1_w_down"][l]),
            _mixer_units(inp["w_in"][l], inp["w_out"][l]),
            _ffn_units(inp["ffn2_w_gate"][l], inp["ffn2_w_up"][l], inp["ffn2_w_down"][l])]
    return np.concatenate(cols, axis=1)


def _colT(v):
    return np.ascontiguousarray(v.reshape(-1, 128).T)


CB_ONES, CB_BONES, CB_ID, CB_OP0, CB_OP1 = 0, 128, 256, 384, 512
CB_MCUR, CB_MPRV, CB_GM, CB_TRIL = 640, 1152, 1664, 2176
CB_END = 2688
CF_UC, CF_MGT = 0, 128
CF_END = 256


def _consts():
    cb = np.zeros((128, CB_END), np.float32)
    i = np.arange(128)
    k, q = i[:, None], i[None, :]
    cb[:, CB_ONES:CB_ONES + 128] = 1.0
    cb[:, CB_BONES:CB_BONES + 128] = (k // 64 == q // 64)
    cb[:, CB_ID:CB_ID + 128] = (k == q)
    cb[:, CB_OP0:CB_OP0 + 64] = 1.0
    cb[:, CB_OP1 + 64:CB_OP1 + 128] = 1.0
    mcur = np.where(q >= k, 0.0, NEG)
    mprv = np.where(k > q, 0.0, NEG)
    same = (k // 64 == q // 64)
    gm = ((k <= q) & same).astype(np.float32)
    tril = (k <= q).astype(np.float32)
    cb[:, CB_MCUR:CB_MCUR + 512] = np.tile(mcur, (1, 4))
    cb[:, CB_MPRV:CB_MPRV + 512] = np.tile(mprv, (1, 4))
    cb[:, CB_GM:CB_GM + 512] = np.tile(gm, (1, 4))
    cb[:, CB_TRIL:CB_TRIL + 512] = np.tile(tril, (1, 4))
    cf = np.zeros((128, CF_END), np.float32)
    cf[:, CF_UC:CF_UC + 128] = gm
    cf[:, CF_MGT:CF_MGT + 128] = ((k > q) & same)
    return cb, cf


LP_G1, LP_GM, LP_G2 = 0, 8, 16
LP_GQ, LP_GK, LP_GO = 24, 25, 26
LP_SINK = 27
LP_VN = 31
LP_BS = LP_VN + 256
LP_BG = LP_BS + 256
LP_WS = LP_BG + 256
LP_WGU = LP_WS + 512
LP_END = LP_WGU + 256


def _layer_params(inp, l):
    p = np.zeros((128, LP_END), np.float32)
    p[:, LP_G1:LP_G1 + 8] = _colT(inp["ffn1_norm"][l])
    p[:, LP_GM:LP_GM + 8] = _colT(inp["mix_norm"][l])
    p[:, LP_G2:LP_G2 + 8] = _colT(inp["ffn2_norm"][l])
    p[:, LP_GQ] = np.tile(inp["attn_q_norm"][l], 2)
    p[:, LP_GK] = np.tile(inp["attn_k_norm"][l], 2)
    p[:, LP_GO] = np.tile(inp["gla_out_norm"][l], 2)
    sk = inp["attn_sinks"][l]
    p[0:64, LP_SINK:LP_SINK + 4] = sk[None, 0:4]
    p[64:128, LP_SINK:LP_SINK + 4] = sk[None, 4:8]
    p[:, LP_VN:LP_VN + 256] = inp["sgu_v_norm"][l][None, :]
    bs = inp["sgu_b"][l]
    for j in range(2):
        for gg in range(2):
            p[gg * 64:(gg + 1) * 64, LP_BS + j * 128:LP_BS + (j + 1) * 128] = bs[2 * j + gg][None, :]
    p[:, LP_BG:LP_BG + 256] = inp["gla_b_gate"][l][None, :]
    ws = inp["sgu_w"][l]
    for g in range(4):
        p[:, LP_WS + g * 128:LP_WS + (g + 1) * 128] = ws[g].T
    p[0:16, LP_WGU:LP_WGU + 256] = inp["gla_w_gate_up"][l]
    return p


def build_program(T, TT=1024, depth=DEPTH, stages=("ffn1", "mix", "A", "B", "C", "ffn2"), dbg=99, dbg2=99):
    assert T % TT == 0 and TT % 512 == 0
    NT = T // TT
    NS = TT // 512
    nc = bass.Bass("TRN2", target_bir_lowering=False)
    xT = nc.dram_tensor("xT", [D_MODEL, T], F32, kind="ExternalInput").ap()
    wts = nc.dram_tensor("wts", [128, depth * LAYER_COLS], F32, kind="ExternalInput").ap()
    lpar = nc.dram_tensor("lpar", [128, depth * LP_END], F32, kind="ExternalInput").ap()
    cstb = nc.dram_tensor("cstb", [128, CB_END], F32, kind="ExternalInput").ap()
    cstf = nc.dram_tensor("cstf", [128, CF_END], F32, kind="ExternalInput").ap()
    yT = nc.dram_tensor("yT", [D_MODEL, T], F32, kind="ExternalOutput").ap()
    wsc = nc.dram_tensor("wsc", [128, depth * LAYER_COLS], BF16, kind="Internal").ap()

    with ExitStack() as st:
        P = Prog(nc, st)

        def tens(name, shape, dt):
            return st.enter_context(nc.sbuf_tensor(name, list(shape), dt))

        xres_t = tens("xres", [128, KC, TT], F32)
        xres = [[Buf(f"xres{k}_{s}", xres_t) for s in range(NS)] for k in range(KC)]
        hT_t = tens("hT", [128, KC, TT], BF16)
        hT = [[Buf(f"hT{k}_{s}", hT_t) for s in range(NS)] for k in range(KC)]
        act_raw = tens("act_raw", [128, max(FH * TT // 2, 5120)], F32)
        act_t = act_raw[:, 0:FH * TT // 2].bitcast(BF16).rearrange("p (f t) -> p f t", f=FH)
        actT = [[Buf(f"act{f}_{s}", act_t) for s in range(NS)] for f in range(FH)]
        act_all = [b for row in actT for b in row]

        NSTG, NWB = 3, 4
        stg = [P.sbuf(f"stg{i}", [128, UW], F32) for i in range(NSTG)]
        wbf_t = [tens(f"wbf{i}", [128, UW], BF16) for i in range(NWB)]
        wbf = [[Buf(f"wbf{i}_{j}", wbf_t[i]) for j in range(3)] for i in range(NWB)]
        late_t = list(wbf_t) + [stg[i].t[:, h * (UW // 2):(h + 1) * (UW // 2)].bitcast(BF16) for i in range(NSTG) for h in range(2)]
        late_b = list(wbf) + [[Buf(f"lw{i}_{h}", None)] for i in range(NSTG) for h in range(2)]
        sq_ring = Ring([P.sbuf(f"sq{i}", [128, 512], BF16) for i in range(4)])
        f32_ring = Ring([P.sbuf(f"fs{i}", [128, 512], F32) for i in range(4)])
        ps_ring = Ring([P.psum(f"ps{i}", [128, 512]) for i in range(6)])
        gla_o = [P.psum(f"glao{i}", [128, 512]) for i in range(2)]

        cb = P.sbuf("cb", [128, CB_END], BF16)
        cf = P.sbuf("cf", [128, CF_END], F32)
        lp = P.sbuf("lp", [128, depth * LP_END], F32)
        P.dma(cf[:], cstf[:, :], writes=[cf])
        P.dma(lp[:], lpar[:, :], writes=[lp])
        for i, c0 in enumerate(range(0, CB_END, UW)):
            w = min(UW, CB_END - c0)
            P.dma(stg[i][:, 0:w], cstb[:, c0:c0 + w], writes=[stg[i]])
            P.op("dve", "tensor_copy", out=cb[:, c0:c0 + w], in_=stg[i][:, 0:w], reads=[stg[i]], writes=[cb])
        ones_b = cb.t[:, CB_ONES:CB_ONES + 128]
        bones_b = cb.t[:, CB_BONES:CB_BONES + 128]
        ident_b = cb.t[:, CB_ID:CB_ID + 128]
        opad_b = [cb.t[:, CB_OP0:CB_OP0 + 128], cb.t[:, CB_OP1:CB_OP1 + 128]]
        mcur_b = cb.t[:, CB_MCUR:CB_MCUR + 512]
        mprv_b = cb.t[:, CB_MPRV:CB_MPRV + 512]
        gm_b = cb.t[:, CB_GM:CB_GM + 512]
        tril_b = cb.t[:, CB_TRIL:CB_TRIL + 512]
        uc_f = cf.t[:, CF_UC:CF_UC + 128]
        mgt_f = cf.t[:, CF_MGT:CF_MGT + 128]

        MIX = "mix" in stages
        if MIX:
            qhat = P.sbuf("qhat", [128, 4, 512], BF16)
            khat = [P.sbuf(f"khat{l}", [128, 2, 640], BF16) for l in range(depth)]
            Vp = [P.sbuf(f"Vp{l}", [128, 5, 2, 128], BF16) for l in range(depth)]
            uT = P.sbuf("uT", [128, 2, 512], BF16)
            vhp_t = tens("vhp", [128, 4, 4, 128], BF16)
            vhp = [Buf(f"vhp{b}", vhp_t) for b in range(4)]
            lrT = P.sbuf("lrT", [16, 512], BF16)
            qd = P.sbuf("qd", [128, 2, 512], BF16)
            ki = P.sbuf("ki", [128, 2, 512], BF16)
            kstp_t = tens("kstp", [128, 4, 4, 128], BF16)
            kstp = [Buf(f"kstp{b}", kstp_t) for b in range(4)]
            Vgp_t = tens("Vgp", [128, 4, 4, 128], BF16)
            Vgp = [Buf(f"Vgp{b}", Vgp_t) for b in range(4)]
            mix_t = tens("mixedT", [128, 8, 512], BF16)
            mixA, mixB, mixC = Buf("mixA", mix_t), Buf("mixB", mix_t), Buf("mixC", mix_t)
            pT_ring = Ring([P.sbuf(f"pT{i}", [128, 512], BF16) for i in range(6)])
            aT_ring = Ring([P.sbuf(f"aT{i}", [128, 512], BF16) for i in range(4)])
            small = P.sbuf("small", [128, 16], F32)
            Sst = [[[P.sbuf(f"S{l}_{hp}_{i}", [128, 128], F32) for i in range(2)] for hp in range(2)] for l in range(depth)]
            s_cur = [[0, 0] for _ in range(depth)]
            vn_t = tens("vn", [128, 8, 4], F32)
            vn = [Buf(f"vn{i}", vn_t) for i in range(8)]
            vn_i = [0]
            mhalf = P.sbuf("mhalf", [128, 2], F32)
            Sbw = [[P.sbuf(f"Sbw{hp}_{i}", [128, 128], BF16) for i in range(9)] for hp in range(2)]
            esink = [P.sbuf(f"esink{l}", [128, 512], F32) for l in range(depth)]
            wcT = [P.sbuf(f"wcT{l}", [128, 512], BF16) for l in range(depth)]
            wgu = [P.sbuf(f"wgu{l}", [16, 256], BF16) for l in range(depth)]
            o = 0
            sp_t = act_raw[:, o:o + 1024].rearrange("p (b c) -> p b c", b=4); o += 1024
            eneg_t = act_raw[:, o:o + 1024].rearrange("p (j t) -> p j t", j=2); o += 1024
            epos_t = act_raw[:, o:o + 1024].rearrange("p (j t) -> p j t", j=2); o += 1024
            etm_t = act_raw[:, o:o + 1024].rearrange("p (b c) -> p b c", b=4); o += 1024
            sgate_t = act_raw[:, o:o + 1024].rearrange("p (j t) -> p j t", j=2); o += 1024
            spB = [Buf(f"sp{b}", sp_t) for b in range(4)]
            enegB, eposB = Buf("eneg", eneg_t), Buf("epos", epos_t)
            etmB = [Buf(f"etm{b}", etm_t) for b in range(4)]
            sgateB = Buf("sgate", sgate_t)
            alias_bufs = spB + [enegB, eposB] + etmB + [sgateB]

            for t_ in (vhp_t, kstp_t, Vgp_t):
                P.op("pool", "memset", t_[:], 0.0)
            P.op("pool", "memset", mhalf[:], -0.5, writes=[mhalf])
            for l in range(depth):
                P.op("pool", "memset", khat[l][:], 0.0, writes=[khat[l]])
                P.op("pool", "memset", Vp[l][:], 0.0, writes=[Vp[l]])
                for hp in range(2):
                    P.op("pool", "memset", Sst[l][hp][0][:], 0.0, writes=[Sst[l][hp][0]])
                lo = l * LP_END
                P.op("act", "activation", out=small[:, 0:4], in_=lp[:, lo + LP_SINK:lo + LP_SINK + 4], func=AF.Exp,
                     reads=[lp], writes=[small])
                for c in range(4):
                    P.op("dve", "tensor_scalar", out=esink[l][:, c * 128:(c + 1) * 128], in0=uc_f, scalar1=0.0,
                         scalar2=small[:, c:c + 1], op0=ALU.mult, op1=ALU.add, reads=[cf, small], writes=[esink[l]])
                P.op("dve", "tensor_tensor", out=wcT[l][:], in0=lp[:, lo + LP_WS:lo + LP_WS + 512], in1=tril_b, op=ALU.mult,
                     reads=[lp, cb], writes=[wcT[l]])
                P.op("dve", "tensor_copy", out=wgu[l][:], in_=lp[0:16, lo + LP_WGU:lo + LP_WGU + 256], reads=[lp], writes=[wgu[l]])
            for b in range(4):
                for B_ in (vhp[b], kstp[b], Vgp[b]):
                    B_.last_w = ("pool", P.count["pool"])

        ws = {"n": 0, "tile": 0, "pend": [], "stored": {}, "late": 0}

        class WB:
            def __init__(self, t, bufs):
                self.t, self.bufs = t, bufs

        def wflush(keep):
            while len(ws["pend"]) > keep:
                col0, width, i = ws["pend"].pop(0)
                ws["stored"][col0] = P.dma(wsc[:, col0:col0 + width], wbf_t[i % NWB][:, 0:width], reads=wbf[i % NWB],
                                           sem_buf=wbf[i % NWB][0])

        def tile_units():
            seq = []

            def ffn_u(c):
                for hf in range(2):
                    for f in range(FH):
                        seq.append((c, W1)); c += W1
                    for dc in range(KC):
                        seq.append((c, W2)); c += W2
            for l in range(depth):
                base = l * LAYER_COLS
                if "ffn1" in stages:
                    ffn_u(base)
                if "mix" in stages:
                    for s_ in range(NS):
                        for u in range(N_FMU + N_TMU + N_OUTU):
                            seq.append((base + FFN_COLS + u * UW, UW))
                if "ffn2" in stages:
                    ffn_u(base + FFN_COLS + MIX_COLS)
            return seq

        useq = tile_units()
        LA = 1
        ws["pos"] = 0
        ws["issued"] = 0
        ws["slot"] = {}

        def wissue(p):
            col0, width = useq[p]
            i = ws["n"]
            ws["n"] += 1
            wt, wb = wbf_t[i % NWB], wbf[i % NWB]
            sg = stg[i % NSTG]
            P.dma(sg[:, 0:width], wts[:, col0:col0 + width], writes=[sg])
            c1, c2 = (width // 4) // 64 * 64, (5 * width // 8) // 64 * 64
            P.op("pool", "tensor_copy", out=wt[:, 0:c1], in_=sg[:, 0:c1], reads=[sg], writes=[wb[0]])
            P.op("act", "activation", out=wt[:, c1:c2], in_=sg[:, c1:c2], func=AF.Copy, reads=[sg], writes=[wb[1]])
            P.op("dve", "tensor_copy", out=wt[:, c2:width], in_=sg[:, c2:width], reads=[sg], writes=[wb[2]])
            ws["pend"].append((col0, width, i))
            wflush(2)
            ws["slot"][p] = WB(wt, wb)

        def wnext(col0, width):
            if ws["tile"] == 0:
                p = ws["pos"]
                assert useq[p] == (col0, width), (p, useq[p], col0, width)
                while ws["issued"] < min(p + 1 + LA, len(useq)):
                    wissue(ws["issued"])
                    ws["issued"] += 1
                ws["pos"] += 1
                return ws["slot"].pop(p)
            j = ws["late"] % len(late_t)
            ws["late"] += 1
            wt, wb = late_t[j], late_b[j]
            P.wait_event("sp", ws["stored"][col0])
            P.dma(wt[:, 0:width], wsc[:, col0:col0 + width], writes=wb, sem_buf=wb[-1])
            return WB(wt, wb)

        def mm(ps_ap, lhsT, rhs, first, last, reads, writes, inc_all=False):
            P.op("pe", "matmul", ps_ap, lhsT=lhsT, rhs=rhs, start=first, stop=last,
                 reads=reads, writes=writes, inc=(last or inc_all), skip_self=True)

        def alias(dst, src):
            merged = {}
            for b in src:
                evs = list(b.readers.items())
                if b.last_w is not None:
                    evs.append(("w_" + b.last_w[0], b.last_w))
                for k, ev in evs:
                    if k not in merged or merged[k][1] < ev[1] or merged[k][0] != ev[0]:
                        if k in merged and merged[k][0] != ev[0]:
                            k = k + "_" + ev[0]
                        if k not in merged or merged[k][1] < ev[1]:
                            merged[k] = ev
            for b in dst:
                for k, ev in merged.items():
                    if k not in b.readers or b.readers[k][1] < ev[1]:
                        b.readers[k] = ev

        def rstd_from(pss, width, dim):
            rs = f32_ring.get()
            P.op("act", "activation", out=rs[:, 0:width], in_=pss[:, 0:width], func=AF.Ln, bias=RMS_EPS, scale=1.0 / dim,
                 reads=[pss], writes=[rs])
            P.op("act", "activation", out=rs[:, 0:width], in_=rs[:, 0:width], func=AF.Exp, scale=-0.5, reads=[rs], writes=[rs])
            return rs

        def norm_gen(lcol, s):
            tsl = slice(s * 512, (s + 1) * 512)
            pss = ps_ring.get()
            sqs = {}

            def square(k):
                sqs[k] = sq_ring.get()
                P.op("act", "activation", out=sqs[k][:], in_=xres_t[:, k, tsl], func=AF.Square, reads=[xres[k][s]], writes=[sqs[k]])

            def accum(k):
                mm(pss[:], ones_b, sqs[k][:], k == 0, k == KC - 1, [sqs[k], cb], [pss], inc_all=True)
                Ring.rel(sqs[k])

            for k in range(4):
                square(k)
            yield
            for k in range(4):
                accum(k)
            for k in range(4, 8):
                square(k)
            yield
            for k in range(4, 8):
                accum(k)
            rs = rstd_from(pss, 512, D_MODEL)
            Ring.rel(pss)
            for k in range(KC):
                P.op("dve", "scalar_tensor_tensor", out=hT_t[:, k, tsl], in0=xres_t[:, k, tsl],
                     scalar=lp[:, lcol + k:lcol + k + 1], in1=rs[:], op0=ALU.mult, op1=ALU.mult,
                     reads=[xres[k][s], rs, lp], writes=[hT[k][s]])
            Ring.rel(rs)

        def norm_to_h(lcol, only=None):
            for s in (range(NS) if only is None else [only]):
                for _ in norm_gen(lcol, s):
                    pass

        def ffn(wcol, lcol, normed=False, after_s=None):
            if not normed:
                norm_to_h(lcol)
            normed_late = False
            c = wcol
            for hf in range(2):
                def gate_up(wb, f, s):
                    wv = wb.t[:, 0:W1].rearrange("p (m k c) -> p m k c", m=2, k=KC)
                    tsl = slice(s * 512, (s + 1) * 512)
                    pg = ps_ring.get()
                    pu = ps_ring.get()
                    for m, pp in ((0, pg), (1, pu)):
                        for k in range(KC):
                            mm(pp[:], wv[:, m, k, :], hT_t[:, k, tsl], k == 0, k == KC - 1, wb.bufs + [hT[k][s]], [pp])
                    sg = f32_ring.get()
                    P.op("act", "activation", out=sg[:], in_=pg[:], func=AF.Silu, reads=[pg], writes=[sg])
                    P.op("dve", "tensor_tensor", out=act_t[:, f, tsl], in0=pu[:], in1=sg[:], op=ALU.mult,
                         reads=[pu, sg], writes=[actT[f][s]])
                    Ring.rel(pg, pu, sg)

                NHEAD = 3 if (hf == 0 and NS > 1 and not normed_late) else 0
                head = []
                for f in range(NHEAD):
                    head.append(wnext(c, W1))
                    c += W1
                for s in range(NS):
                    for f in range(NHEAD):
                        gate_up(head[f], f, s)
                for f in range(NHEAD, FH):
                    wb = wnext(c, W1)
                    c += W1
                    for s in range(NS):
                        gate_up(wb, f, s)
                def down(wb, dc, s):
                    wv = wb.t[:, 0:W2].rearrange("p (k c) -> p k c", k=FH)
                    tsl = slice(s * 512, (s + 1) * 512)
                    py = ps_ring.get()
                    for f in range(FH):
                        mm(py[:], wv[:, f, :], act_t[:, f, tsl], f == 0, f == FH - 1, wb.bufs + [actT[f][s]], [py])
                    P.op("dve", "scalar_tensor_tensor", out=xres_t[:, dc, tsl], in0=py[:], scalar=0.5,
                         in1=xres_t[:, dc, tsl], op0=ALU.mult, op1=ALU.add,
                         reads=[py, xres[dc][s]], writes=[xres[dc][s]])
                    Ring.rel(py)

                if hf == 1 and ws["tile"] > 0 and NS > 1:
                    units = []
                    for dc in range(KC):
                        units.append(wnext(c, W2))
                        c += W2
                    gen = None
                    for s in range(NS):
                        for dc in range(KC):
                            down(units[dc], dc, s)
                            if gen is not None and dc >= 1:
                                next(gen, None)
                        if gen is not None:
                            for _ in gen:
                                pass
                        gen = after_s(s) if after_s is not None else None
                    if gen is not None:
                        for _ in gen:
                            pass
                else:
                    for dc in range(KC):
                        wb = wnext(c, W2)
                        c += W2
                        for s in range(NS):
                            down(wb, dc, s)
                    if hf == 1 and after_s is not None:
                        for s in range(NS):
                            for _ in after_s(s):
                                pass

        def mixer_sub(wcol, l, s, first):
            lo = l * LP_END
            t0 = s * 512
            tsl = slice(t0, t0 + 512)
            c = wcol
            kh, vp = khat[l], Vp[l]
            doA, doB, doC = "A" in stages, "B" in stages, "C" in stages
            if not first:
                P.op("pool", "tensor_copy", out=kh[:, :, 0:128], in_=kh[:, :, 512:640], reads=[kh], writes=[kh])
                P.op("pool", "tensor_copy", out=vp[:, 0, :, :], in_=vp[:, 4, :, :], reads=[vp], writes=[vp])

            def diag2(t3, j):
                flat = t3.rearrange("p h c -> p (h c)")
                if j == 0:
                    return flat[:, 0:384].rearrange("p (a c) -> p a c", c=192)[:, :, 0:64]
                return flat[:, 128:512].rearrange("p (a c) -> p a c", c=192)[:, :, 128:192]

            def proj_fm(wb, j, M=128):
                wv = wb.t[:, 0:UW].rearrange("p (j k c) -> p j k c", j=2, k=KC)
                ps = ps_ring.get()
                for k in range(KC):
                    mm(ps[0:M, :], wv[:, j, k, 0:M], hT_t[:, k, tsl], k == 0, k == KC - 1, wb.bufs + [hT[k][s]], [ps])
                return ps

            def qk_part1(ps):
                sq = sq_ring.get()
                P.op("act", "activation", out=sq[:], in_=ps[:], func=AF.Square, reads=[ps], writes=[sq])
                return sq

            def qk_part2(ps, sq, is_k, cq):
                ps2 = ps_ring.get()
                mm(ps2[:], bones_b, sq[:], True, True, [sq, cb], [ps2])
                Ring.rel(sq)
                rs = rstd_from(ps2, 512, 64)
                Ring.rel(ps2)
                if is_k:
                    for g in range(2):
                        r = slice(g * 64, (g + 1) * 64)
                        P.op("dve", "scalar_tensor_tensor", out=kh[r, g, 128:640], in0=ps[r, :],
                             scalar=lp[r, lo + LP_GK:lo + LP_GK + 1], in1=rs[r, :], op0=ALU.mult, op1=ALU.mult,
                             reads=[ps, rs, lp], writes=[kh])
                else:
                    P.op("dve", "scalar_tensor_tensor", out=qhat[:, cq, :], in0=ps[:],
                         scalar=lp[:, lo + LP_GQ:lo + LP_GQ + 1], in1=rs[:], op0=ALU.mult, op1=ALU.mult,
                         reads=[ps, rs, lp], writes=[qhat])
                Ring.rel(ps, rs)

            def qk_norm(ps, is_k, cq):
                qk_part2(ps, qk_part1(ps), is_k, cq)

            if dbg < 1:
                return
            wb = wnext(c, UW); c += UW
            ps = proj_fm(wb, 0, M=16)
            P.op("act", "activation", out=lrT[:, :], in_=ps[0:16, :], func=AF.Copy, reads=[ps], writes=[lrT])
            Ring.rel(ps)
            ps = proj_fm(wb, 1)
            if doA:
                qk_norm(ps, True, 0)
            else:
                Ring.rel(ps)
            if dbg < 2:
                return
            if doC:
                pls, zs = [], []
                for b in range(4):
                    pl = ps_ring.get()
                    mm(pl[:, 0:256], lrT[0:16, b * 128:(b + 1) * 128], wgu[l][:, :], True, True, [lrT, wgu[l]], [pl])
                    pls.append(pl)
                for b in range(4):
                    z = f32_ring.get()
                    P.op("dve", "tensor_tensor", out=z[:, 0:256], in0=pls[b][:, 0:256], in1=lp[:, lo + LP_BG:lo + LP_BG + 256],
                         op=ALU.add, reads=[pls[b], lp], writes=[z])
                    Ring.rel(pls[b])
                    zs.append(z)
                for b in range(4):
                    P.op("act", "activation", out=zs[b][:, 0:256], in_=zs[b][:, 0:256], func=AF.Exp, scale=-1.0, reads=[zs[b]], writes=[zs[b]])
                for b in range(4):
                    P.op("act", "activation", out=sp_t[:, b, :], in_=zs[b][:, 0:256], func=AF.Ln, bias=1.0, reads=[zs[b]], writes=[spB[b]])
                    Ring.rel(zs[b])
            qpend = [None]
            for u in range(1, N_FMU):
                if u == 3:
                    if qpend[0] is not None:
                        qk_part2(*qpend[0])
                        qpend[0] = None
                    if doC:
                        pcs = []
                        for b in range(4):
                            pc = ps_ring.get()
                            for j in range(2):
                                mm(pc[:, j * 128:(j + 1) * 128], sp_t[:, b, j * 128:(j + 1) * 128], uc_f, True, True, [spB[b], cf], [pc], inc_all=True)
                            mm(pc[:, 256:512], mgt_f, sp_t[:, b, :], True, True, [spB[b], cf], [pc], inc_all=True)
                            pcs.append(pc)
                        for b in range(4):
                            bsl = slice(b * 128, (b + 1) * 128)
                            pcv = pcs[b].t[:, 0:256].rearrange("p (j t) -> p j t", j=2)
                            P.op("act", "activation", out=eneg_t[:, :, bsl], in_=pcv, func=AF.Exp, scale=-1.0 / 16, reads=[pcs[b]], writes=[enegB])
                            P.op("act", "activation", out=epos_t[:, :, bsl], in_=pcv, func=AF.Exp, scale=1.0 / 16, reads=[pcs[b]], writes=[eposB])
                            P.op("act", "activation", out=etm_t[:, b, :], in_=pcs[b][:, 256:512], func=AF.Exp, scale=-1.0 / 16, reads=[pcs[b]], writes=[etmB[b]])
                            Ring.rel(pcs[b])
                wb = wnext(c, UW); c += UW
                for j in range(2):
                    ci = 2 * u + j
                    need = (ci <= 5 and doA) or (ci in (6, 7) and doB) or (ci >= 8 and doC)
                    if not need:
                        continue
                    ps = proj_fm(wb, j)
                    if ci <= 5:
                        cur = (ps, qk_part1(ps), False, ci - 2)
                        if qpend[0] is not None:
                            qk_part2(*qpend[0])
                        qpend[0] = cur
                        continue
                    if qpend[0] is not None:
                        qk_part2(*qpend[0])
                        qpend[0] = None
                    if ci <= 7:
                        P.op("act", "activation", out=uT[:, ci - 6, :], in_=ps[:], func=AF.Gelu_apprx_tanh, reads=[ps], writes=[uT])
                        Ring.rel(ps)
                    elif ci <= 9:
                        P.op("dve", "scalar_tensor_tensor", out=qd[:, ci - 8, :], in0=ps[:], scalar=0.125, in1=eneg_t[:, ci - 8, :],
                             op0=ALU.mult, op1=ALU.mult, reads=[ps, enegB], writes=[qd])
                        Ring.rel(ps)
                    elif ci <= 11:
                        P.op("dve", "tensor_tensor", out=ki[:, ci - 10, :], in0=ps[:], in1=epos_t[:, ci - 10, :], op=ALU.mult,
                             reads=[ps, eposB], writes=[ki])
                        Ring.rel(ps)
                    else:
                        P.op("act", "activation", out=sgate_t[:, ci - 12, :], in_=ps[:], func=AF.Silu, reads=[ps], writes=[sgateB])
                        Ring.rel(ps)
            if qpend[0] is not None:
                qk_part2(*qpend[0])
                qpend[0] = None
            if dbg < 3:
                return
            for u in range(N_TMU // 2):
                wbs = [wnext(c, UW), wnext(c + UW, UW)]
                c += 2 * UW
                wvs = [w_.t[:, 0:UW].rearrange("p (k c) -> p k c", k=4) for w_ in wbs]
                for b in range(4):
                    pieces = [pi for pi in range(4 * u, 4 * u + 4) if pi < 7 and pi in _PCS and ((pi == 0 and doA) or (pi in (1, 2) and doB) or (pi >= 3 and doC))]
                    if not pieces:
                        continue
                    ps = ps_ring.get()
                    for k in range(KC):
                        mm(ps[:, :], hT_t[:, k, t0 + b * 128:t0 + (b + 1) * 128], wvs[k // 4][:, k % 4, :], k == 0, k == KC - 1,
                           wbs[k // 4].bufs + [hT[k][s]], [ps])
                    pieces = sorted(pieces, key=lambda q_: 0 if q_ >= 3 else 1)
                    dve_first = [kstp[b]] if any(q_ >= 3 for q_ in pieces) else []
                    for pi in pieces:
                        co = (pi % 4) * 128
                        if pi == 0:
                            for g in range(2):
                                P.op("act", "activation", out=vp[:, 1 + b, g, g * 64:(g + 1) * 64], in_=ps[:, co + g * 64:co + (g + 1) * 64],
                                     func=AF.Copy, reads=[ps] + dve_first, writes=[vp])
                        elif pi <= 2:
                            j = pi - 1
                            vg = f32_ring.get()
                            P.op("act", "activation", out=vg[:, 0:128], in_=ps[:, co:co + 128], func=AF.Gelu_apprx_tanh, reads=[ps] + dve_first, writes=[vg])
                            vs = vn_i[0] % 8
                            vn_i[0] += 1
                            for gg in range(2):
                                P.op("act", "activation", out=vg[:, 256 + gg * 64:256 + (gg + 1) * 64], in_=vg[:, gg * 64:(gg + 1) * 64], func=AF.Square,
                                     accum_out=vn_t[:, vs, gg:gg + 1], reads=[vg], writes=[vg, vn[vs]])
                            P.op("pool", "tensor_scalar", out=vn_t[:, vs, 2:4], in0=vn_t[:, vs, 0:2], scalar1=1.0 / 64, scalar2=RMS_EPS,
                                 op0=ALU.mult, op1=ALU.add, reads=[vn[vs]], writes=[vn[vs]])
                            P.op("pool", "tensor_tensor", out=vn_t[:, vs, 2:4], in0=vn_t[:, vs, 2:4], in1=mhalf[:, :], op=ALU.pow,
                                 reads=[vn[vs], mhalf], writes=[vn[vs]])
                            for gg in range(2):
                                g4 = 2 * j + gg
                                P.op("dve", "scalar_tensor_tensor", out=vhp_t[:, b, g4, gg * 64:(gg + 1) * 64], in0=vg[:, gg * 64:(gg + 1) * 64],
                                     scalar=vn_t[:, vs, 2 + gg:3 + gg], in1=lp[:, lo + LP_VN + g4 * 64:lo + LP_VN + (g4 + 1) * 64],
                                     op0=ALU.mult, op1=ALU.mult, reads=[vg, vn[vs], lp], writes=[vhp[b]])
                            Ring.rel(vg)
                        elif pi <= 4:
                            j = pi - 3
                            P.op("dve", "tensor_tensor", out=diag2(kstp_t[:, b, :, :], j), in0=ps[:, co:co + 128].rearrange("p (a c) -> p a c", a=2),
                                 in1=etm_t[:, b, j * 128:(j + 1) * 128].rearrange("p (a c) -> p a c", a=2), op=ALU.mult,
                                 reads=[ps, etmB[b]], writes=[kstp[b]])
                        else:
                            j = pi - 5
                            P.op("dve", "tensor_copy", out=diag2(Vgp_t[:, b, :, :], j), in_=ps[:, co:co + 128].rearrange("p (a c) -> p a c", a=2),
                                 reads=[ps], writes=[Vgp[b]])
                    Ring.rel(ps)
            if dbg < 4:
                return
            gsq = [None, None]

            def gla_norm_p1():
                for hp in range(2):
                    gsq[hp] = sq_ring.get()
                    P.op("act", "activation", out=gsq[hp][:], in_=gla_o[hp][:], func=AF.Square, reads=[gla_o[hp]], writes=[gsq[hp]])

            def gla_norm_p2():
                for hp in range(2):
                    po = gla_o[hp]
                    ps2 = ps_ring.get()
                    mm(ps2[:], bones_b, gsq[hp][:], True, True, [gsq[hp], cb], [ps2])
                    Ring.rel(gsq[hp])
                    rs = rstd_from(ps2, 512, 64)
                    Ring.rel(ps2)
                    tmp = f32_ring.get()
                    P.op("dve", "scalar_tensor_tensor", out=tmp[:], in0=po[:], scalar=lp[:, lo + LP_GO:lo + LP_GO + 1], in1=rs[:],
                         op0=ALU.mult, op1=ALU.mult, reads=[po, rs, lp], writes=[tmp])
                    P.op("dve", "tensor_tensor", out=mix_t[:, 6 + hp, :], in0=tmp[:], in1=sgate_t[:, hp, :], op=ALU.mult,
                         reads=[tmp, sgateB], writes=[mixC])
                    Ring.rel(rs, tmp)

            aTs = [None] * 4
            if doC:
                for hp in range(2):
                    sc_ = Sst[l][hp][s_cur[l][hp]]
                    P.op("pool", "tensor_copy", out=Sbw[hp][0][:, :], in_=sc_[:, :], reads=[sc_], writes=[Sbw[hp][0]])
                for b in range(4):
                    bsl = slice(b * 128, (b + 1) * 128)
                    pAs = [ps_ring.get(), ps_ring.get()]
                    aT = aT_ring.get()
                    aTs[b] = aT
                    aT4 = aT.t[:, :].rearrange("p (j h t) -> p j h t", j=2, h=2)
                    for hh in range(2):
                        r = slice(hh * 64, (hh + 1) * 64)
                        for j in range(2):
                            mm(pAs[hh][:, j * 128:(j + 1) * 128], ki[r, j, bsl], qd[r, j, bsl], True, True, [ki, qd], [pAs[hh]], inc_all=True)
                        P.op("dve", "tensor_tensor", out=aT4[:, :, hh, :], in0=pAs[hh].t[:, 0:256].rearrange("p (j t) -> p j t", j=2),
                             in1=gm_b[:, 0:256].rearrange("p (j t) -> p j t", j=2), op=ALU.mult, reads=[pAs[hh], cb], writes=[aT])
                    Ring.rel(*pAs)
                    pdl = [gla_o[0], gla_o[1]] if b % 2 == 1 else [ps_ring.get(), ps_ring.get()]
                    for cc in range(2):
                        r = slice(cc * 64, (cc + 1) * 64)
                        for hp in range(2):
                            osl = slice(hp * 128, (hp + 1) * 128)
                            mm(pdl[cc][:, osl], kstp_t[r, b, 2 * hp, :], Vgp_t[r, b, 2 * hp, :], True, False, [kstp[b], Vgp[b]], [pdl[cc]], inc_all=True)
                            mm(pdl[cc][:, osl], kstp_t[r, b, 2 * hp + 1, :], Vgp_t[r, b, 2 * hp + 1, :], False, True, [kstp[b], Vgp[b]], [pdl[cc]], inc_all=True)
                    for cc in range(2):
                        tk = b * 128 + cc * 64
                        for hp in range(2):
                            so = Sst[l][hp][s_cur[l][hp]]
                            s_cur[l][hp] ^= 1
                            sn = Sst[l][hp][s_cur[l][hp]]
                            P.op("dve", "scalar_tensor_tensor", out=sn[:, :], in0=so[:, :], scalar=eneg_t[:, hp, tk + 63:tk + 64],
                                 in1=pdl[cc][:, hp * 128:(hp + 1) * 128], op0=ALU.mult, op1=ALU.add, reads=[so, enegB, pdl[cc]], writes=[sn])
                            sbn = Sbw[hp][2 * b + cc + 1]
                            P.op("pool", "tensor_copy", out=sbn[:, :], in_=sn[:, :], reads=[sn], writes=[sbn])
                    if b % 2 == 0:
                        Ring.rel(*pdl)

            for b in range(4):
                bsl = slice(b * 128, (b + 1) * 128)
                if doB:
                    pz = ps_ring.get()
                    for j in range(2):
                        for gg in range(2):
                            g4 = 2 * j + gg
                            mm(pz[:, j * 128:(j + 1) * 128], vhp_t[:, b, g4, :], wcT[l][:, g4 * 128:(g4 + 1) * 128], gg == 0, gg == 1,
                               [vhp[b], wcT[l]], [pz], inc_all=True)
                    tz = f32_ring.get()
                    P.op("dve", "tensor_tensor", out=tz[:, 0:256], in0=pz[:, 0:256], in1=lp[:, lo + LP_BS:lo + LP_BS + 256], op=ALU.add,
                         reads=[pz, lp], writes=[tz])
                    P.op("dve", "tensor_tensor", out=mix_t[:, 4:6, bsl], in0=tz.t[:, 0:256].rearrange("p (j t) -> p j t", j=2),
                         in1=uT[:, :, bsl], op=ALU.mult, reads=[tz, uT], writes=[mixB])
                    Ring.rel(pz, tz)

            def gla_out(b):
                bsl = slice(b * 128, (b + 1) * 128)
                aT = aTs[b]
                for hp in range(2):
                    po = gla_o[hp]
                    mm(po[:, bsl], Vgp_t[:, b, 2 * hp, :], aT[:, (2 * hp) * 128:(2 * hp + 1) * 128], True, False, [Vgp[b], aT], [po], inc_all=True)
                    mm(po[:, bsl], Vgp_t[:, b, 2 * hp + 1, :], aT[:, (2 * hp + 1) * 128:(2 * hp + 2) * 128], False, False, [Vgp[b], aT], [po], inc_all=True)
                    for cc in range(2):
                        tk = b * 128 + cc * 64
                        sbc = Sbw[hp][2 * b + cc]
                        mm(po[:, tk:tk + 64], sbc[:, :], qd[:, hp, tk:tk + 64], False, cc == 1, [sbc, qd], [po], inc_all=True)
                Ring.rel(aT)

            def attn_block(b):
                bsl = slice(b * 128, (b + 1) * 128)
                if True:
                    pts = []
                    for g in range(2):
                        for cur in (0, 1):
                            if first and b == 0 and not cur:
                                continue
                            pS = ps_ring.get()
                            kb = b + cur
                            mm(pS[:], kh[:, g, kb * 128:(kb + 1) * 128], qhat[:, :, bsl], True, False, [kh, qhat], [pS], inc_all=True)
                            mm(pS[:], ident_b, mcur_b if cur else mprv_b, False, True, [cb], [pS])
                            pt = pT_ring.get()
                            P.op("act", "activation", out=pt[:], in_=pS[:], func=AF.Exp, scale=0.125, reads=[pS], writes=[pt])
                            Ring.rel(pS)
                            pts.append((g, kb, pt))
                    if dbg < 6:
                        Ring.rel(*[p_[2] for p_ in pts])
                        return
                    po = ps_ring.get()
                    pd = ps_ring.get()
                    for i, (g, kb, pt) in enumerate(pts):
                        mm(po[:], vp[:, kb, g, :], pt[:], i == 0, i == len(pts) - 1, [vp, pt], [po], inc_all=True)
                    for i, (g, kb, pt) in enumerate(pts):
                        mm(pd[:], opad_b[g], pt[:], i == 0, i == len(pts) - 1, [cb, pt], [pd], inc_all=True)
                    Ring.rel(*[p_[2] for p_ in pts])
                    if dbg < 7:
                        Ring.rel(po, pd)
                        return
                    dt = f32_ring.get()
                    P.op("dve", "tensor_tensor", out=dt[:], in0=pd[:], in1=esink[l][:], op=ALU.add, reads=[pd, esink[l]], writes=[dt])
                    P.op("dve", "reciprocal", out=dt[:], in_=dt[:], reads=[dt], writes=[dt])
                    P.op("dve", "tensor_tensor", out=mix_t[:, 0:4, bsl], in0=po.t[:, :].rearrange("p (c q) -> p c q", c=4),
                         in1=dt.t[:, :].rearrange("p (c q) -> p c q", c=4), op=ALU.mult, reads=[po, dt], writes=[mixA])
                    Ring.rel(po, pd, dt)

            for b in range(4):
                if doA and b < 3:
                    attn_block(b)
                if doC:
                    gla_out(b)
            if doC:
                gla_norm_p1()
            if doA:
                attn_block(3)
            if doC:
                gla_norm_p2()
            if dbg < 8:
                return
            for u in range(N_OUTU):
                wb = wnext(c, UW); c += UW
                wv = wb.t[:, 0:UW].rearrange("p (j k c) -> p j k c", j=2, k=KC)
                for j in range(2):
                    dc = 2 * u + j
                    ks = [k for k in range(KC) if (k < 4 and doA) or (k in (4, 5) and doB) or (k >= 6 and doC)]
                    if not ks:
                        continue
                    ps = ps_ring.get()
                    for i, k in enumerate(ks):
                        mm(ps[:], wv[:, j, k, :], mix_t[:, k, :], i == 0, i == len(ks) - 1,
                           wb.bufs + [mixA if k < 4 else (mixB if k < 6 else mixC)], [ps])
                    P.op("dve", "tensor_tensor", out=xres_t[:, dc, tsl], in0=ps[:], in1=xres_t[:, dc, tsl], op=ALU.add,
                         reads=[ps, xres[dc][s]], writes=[xres[dc][s]])
                    Ring.rel(ps)

        def mixer(wcol, l, it, normed=False):
            if not normed:
                norm_to_h(l * LP_END + LP_GM)
            alias(alias_bufs, act_all)
            for s in range(NS):
                mixer_sub(wcol, l, s, first=(it == 0 and s == 0))
                if "ffn2" in stages and s < NS - 1:
                    norm_to_h(l * LP_END + LP_G2, only=s)
            if "ffn2" in stages:
                norm_to_h(l * LP_END + LP_G2, only=NS - 1)
            alias(act_all, alias_bufs)

        xT_v = xT.rearrange("(k p) t -> p k t", p=128)
        yT_v = yT.rearrange("(k p) t -> p k t", p=128)
        out_evs = []
        F1, F2 = "ffn1" in stages, "ffn2" in stages
        for it in range(NT):
            t0 = it * TT
            ws["tile"] = it
            for s_ in range(NS):
                for k in range(KC):
                    P.dma(xres_t[:, k, s_ * 512:(s_ + 1) * 512], xT_v[:, k, t0 + s_ * 512:t0 + (s_ + 1) * 512],
                          writes=[xres[k][s_]])

            def store(s_, t0=t0):
                for k in range(KC):
                    out_evs.append(P.dma(yT_v[:, k, t0 + s_ * 512:t0 + (s_ + 1) * 512], xres_t[:, k, s_ * 512:(s_ + 1) * 512],
                                         reads=[xres[k][s_]]))
                return iter(())

            stored = False
            for l in range(depth):
                base = l * LAYER_COLS
                lo_ = l * LP_END
                if F1:
                    nxt = (lambda s_, lo_=lo_: norm_gen(lo_ + LP_GM, s_)) if MIX else None
                    ffn(base, lo_ + LP_G1, normed=(l > 0 and F2), after_s=nxt)
                if MIX:
                    mixer(base + FFN_COLS, l, it, normed=F1)
                if F2:
                    if l + 1 < depth and F1:
                        nxt = (lambda s_, lo2=(l + 1) * LP_END: norm_gen(lo2 + LP_G1, s_))
                    elif l + 1 == depth:
                        nxt = store
                        stored = True
                    else:
                        nxt = None
                    ffn(base + FFN_COLS + MIX_COLS, lo_ + LP_G2, normed=MIX, after_s=nxt)
            wflush(0)
            if it == 0 and NT > 1:
                alias([b_ for bl in late_b[NWB:] for b_ in bl], stg)
            if not stored:
                for s_ in range(NS):
                    store(s_)
        for ev in out_evs:
            P.wait_event("sp", ev)
        P.emit()
    return nc


_NC_CACHE = {}


def kernel(**inp):
    inp = {k: np.asarray(v) for k, v in inp.items()}
    x = inp["x"]
    B, T, _ = x.shape
    key = (T,)
    if key not in _NC_CACHE:
        _NC_CACHE[key] = build_program(T)
    nc = _NC_CACHE[key]
    wts = np.concatenate([_pack_layer_weights(inp, l) for l in range(DEPTH)], axis=1)
    lpar = np.concatenate([_layer_params(inp, l) for l in range(DEPTH)], axis=1)
    cb, cf = _consts()
    in_maps = [{"xT": np.ascontiguousarray(x[b].T), "wts": wts, "lpar": lpar, "cstb": cb, "cstf": cf} for b in range(B)]
    res = run_bass_kernel_spmd(nc, in_maps, core_ids=list(range(B)))
    return np.stack([np.ascontiguousarray(res.results[b]["yT"].T) for b in range(B)], axis=0)
```

```python
import numpy as np
from contextlib import ExitStack
import concourse.bass as bass
import concourse.mybir as mybir
from concourse.bass_utils import run_bass_kernel_spmd

F32 = mybir.dt.float32
BF16 = mybir.dt.bfloat16
AF = mybir.ActivationFunctionType
ALU = mybir.AluOpType
AX = mybir.AxisListType

D_MODEL = 1024
DEPTH = 2
D_FF = 2816
KC = D_MODEL // 128
FC = D_FF // 128
FH = FC // 2
RMS_EPS = 1e-6
UW = 2048
W1 = 2 * KC * 128
W2 = FH * 128
N_FM = 14
N_FMU = N_FM // 2
N_TMU = 4
N_OUTU = 4
FFN_COLS = 2 * (FH * W1 + KC * W2)
MIX_COLS = (N_FMU + N_TMU + N_OUTU) * UW
LAYER_COLS = 2 * FFN_COLS + MIX_COLS
NEG = -30000.0
import os as _os
_PCS = [int(v) for v in _os.environ.get('PCS', '0,1,2,3,4,5,6').split(',')]


class Buf:
    __slots__ = ("name", "t", "last_w", "readers", "dma_sem", "dma_cnt", "free")

    def __init__(self, name, t):
        self.name = name
        self.t = t
        self.last_w = None
        self.readers = {}
        self.dma_sem = None
        self.dma_cnt = 0
        self.free = True

    def __getitem__(self, idx):
        return self.t[idx]


class Ring:
    def __init__(self, bufs):
        self.bufs = bufs
        self.i = 0

    def get(self):
        b = self.bufs[self.i % len(self.bufs)]
        assert b.free, f"ring buffer {b.name} still live"
        b.free = False
        self.i += 1
        return b

    @staticmethod
    def rel(*bs):
        for b in bs:
            b.free = True


class Prog:
    ENGS = ("pe", "act", "dve", "pool", "sp")

    def __init__(self, nc, stack):
        self.nc = nc
        self.stack = stack
        self.items = {e: [] for e in self.ENGS}
        self.count = {e: 0 for e in self.ENGS}
        self.pending = {e: False for e in self.ENGS}
        self.seen = {e: {} for e in self.ENGS}
        self.hist = {}
        self.sems = {}
        for e in self.ENGS:
            self.sems[e] = stack.enter_context(nc.semaphore("s_" + e))

    def sbuf(self, name, shape, dtype):
        return Buf(name, self.stack.enter_context(self.nc.sbuf_tensor(name, list(shape), dtype)))

    def psum(self, name, shape, dtype=F32):
        return Buf(name, self.stack.enter_context(self.nc.psum_tensor(name, list(shape), dtype)))

    def _dma_sem(self, b):
        if b.dma_sem is None:
            key = "d_" + b.name
            self.sems[key] = self.stack.enter_context(self.nc.semaphore(key))
            b.dma_sem = key
        return b.dma_sem

    def _waits(self, eng, reads, writes, skip_self):
        deps = []
        for b in reads:
            if b.last_w is not None:
                deps.append(b.last_w)
        for b in writes:
            if b.last_w is not None:
                deps.append(b.last_w)
            deps.extend(b.readers.values())
        seen = self.seen[eng]
        best = {}
        for k, v in deps:
            if skip_self and k == eng:
                continue
            if seen.get(k, 0) >= v:
                continue
            if best.get(k, 0) < v:
                best[k] = v
        kept = []
        for k, v in sorted(best.items(), key=lambda kv: -len(self.hist.get(kv, ()))):
            if seen.get(k, 0) >= v:
                continue
            kept.append((k, v))
            seen[k] = v
            for k2, v2 in self.hist.get((k, v), {}).items():
                if seen.get(k2, 0) < v2:
                    seen[k2] = v2
        return kept

    def op(self, eng, name, *args, reads=(), writes=(), inc=True, skip_self=False, **kw):
        fn = (name, args, kw)
        waits = self._waits(eng, reads, writes, skip_self)
        if inc:
            self.count[eng] += 1
            ev = (eng, self.count[eng])
            self.pending[eng] = False
            self.hist[ev] = dict(self.seen[eng])
        else:
            ev = (eng, self.count[eng] + 1)
            self.pending[eng] = True
        self.items[eng].append((waits, fn, (eng, 1) if inc else None))
        for b in writes:
            b.last_w = ev
            b.readers = {}
        for b in reads:
            if b not in writes:
                b.readers[eng] = ev
        return ev

    def dma(self, out, in_, reads=(), writes=(), sem_buf=None, queue="sp"):
        fn = ("dma_start", (), dict(out=out, in_=in_))
        sb = sem_buf or (writes[0] if writes else reads[0])
        key = self._dma_sem(sb)
        waits = self._waits(queue, reads, writes, False)
        sb.dma_cnt += 16
        ev = (key, sb.dma_cnt)
        self.hist[ev] = dict(self.seen[queue])
        self.items[queue].append((waits, fn, (key, 16)))
        for b in writes:
            b.last_w = ev
            b.readers = {}
        for b in reads:
            if b not in writes:
                b.readers["q_" + key] = ev
        return ev

    def wait_event(self, eng, ev):
        k, v = ev
        if self.seen[eng].get(k, 0) < v:
            self.seen[eng][k] = v
            self.items[eng].append(([(k, v)], None, None))

    def emit(self):
        nc = self.nc
        for e in self.ENGS:
            assert not self.pending[e], f"engine {e} has pending un-inc'd instrs"
        with nc.Block() as block:
            def run(engname):
                def body(engine):
                    for waits, fn, inc in self.items[engname]:
                        fuse = (fn is not None and waits and "accum_out" not in fn[2])
                        for k, v in (waits[:-1] if fuse else waits):
                            engine.wait_ge(self.sems[k], v)
                        if fn is None:
                            continue
                        r = getattr(engine, fn[0])(*fn[1], **fn[2])
                        if fuse:
                            r = r._wait_ge(self.sems[waits[-1][0]], waits[-1][1])
                        if inc is not None:
                            r.then_inc(self.sems[inc[0]], inc[1])
                return body
            block.sync(run("sp"))
            block.tensor(run("pe"))
            block.scalar(run("act"))
            block.vector(run("dve"))
            block.gpsimd(run("pool"))


def _ffn_units(wg, wu, wd):
    g = wg.reshape(KC, 128, FC, 128)
    u = wu.reshape(KC, 128, FC, 128)
    d = wd.reshape(FC, 128, KC, 128)
    parts = []
    for hf in range(2):
        for f in range(FH):
            fg = hf * FH + f
            gu = np.stack([g[:, :, fg, :], u[:, :, fg, :]], axis=0)
            parts.append(gu.transpose(2, 0, 1, 3).reshape(128, W1))
        for dc in range(KC):
            blk = d[hf * FH:(hf + 1) * FH, :, dc, :]
            parts.append(blk.transpose(1, 0, 2).reshape(128, W2))
    return np.concatenate(parts, axis=1)


def _fm_unit(ca, cb):
    a = np.stack([ca.reshape(KC, 128, 128), cb.reshape(KC, 128, 128)], axis=0)
    return a.transpose(2, 0, 1, 3).reshape(128, UW)


def _tm_unit(pieces4, half):
    a = np.concatenate(pieces4, axis=1).reshape(KC, 128, 512)[4 * half:4 * half + 4]
    return a.transpose(1, 0, 2).reshape(128, UW)


def _mixer_units(w_in, w_out):
    z = np.zeros((D_MODEL, 128), np.float32)
    aq, ak, av = w_in[:, 0:512], w_in[:, 512:640], w_in[:, 640:768]
    su, sv = w_in[:, 768:1024], w_in[:, 1024:1280]
    cq, ck, cv, cg = w_in[:, 1280:1536], w_in[:, 1536:1792], w_in[:, 1792:2048], w_in[:, 2048:2304]
    clr = z.copy()
    clr[:, 0:16] = w_in[:, 2304:2320]
    qc = [np.concatenate([aq[:, c * 64:(c + 1) * 64], aq[:, (4 + c) * 64:(5 + c) * 64]], axis=1) for c in range(4)]
    fm = [clr, ak, qc[0], qc[1], qc[2], qc[3], su[:, 0:128], su[:, 128:256],
          cq[:, 0:128], cq[:, 128:256], ck[:, 0:128], ck[:, 128:256], cg[:, 0:128], cg[:, 128:256]]
    tm = [av, sv[:, 0:128], sv[:, 128:256], ck[:, 0:128], ck[:, 128:256], cv[:, 0:128], cv[:, 128:256], z]
    perm = []
    for c in range(4):
        perm += list(range(c * 64, (c + 1) * 64)) + list(range((4 + c) * 64, (5 + c) * 64))
    perm += list(range(512, 1024))
    wo = w_out[np.array(perm), :]
    parts = [_fm_unit(fm[2 * u], fm[2 * u + 1]) for u in range(N_FMU)]
    parts += [_tm_unit(tm[4 * (u // 2):4 * (u // 2) + 4], u % 2) for u in range(N_TMU)]
    parts += [_fm_unit(wo[:, (2 * u) * 128:(2 * u + 1) * 128], wo[:, (2 * u + 1) * 128:(2 * u + 2) * 128]) for u in range(N_OUTU)]
    return np.concatenate(parts, axis=1)


def _pack_layer_weights(inp, l):
    cols = [_ffn_units(inp["ffn1_w_gate"][l], inp["ffn1_w_up"][l], inp["ffn1_w_down"][l]),
            _mixer_units(inp["w_in"][l], inp["w_out"][l]),
            _ffn_units(inp["ffn2_w_gate"][l], inp["ffn2_w_up"][l], inp["ffn2_w_down"][l])]
    return np.concatenate(cols, axis=1)


def _colT(v):
    return np.ascontiguousarray(v.reshape(-1, 128).T)


CB_ONES, CB_BONES, CB_ID, CB_OP0, CB_OP1 = 0, 128, 256, 384, 512
CB_MCUR, CB_MPRV, CB_GM, CB_TRIL = 640, 1152, 1664, 2176
CB_END = 2688
CF_UC, CF_MGT = 0, 128
CF_END = 256


def _consts():
    cb = np.zeros((128, CB_END), np.float32)
    i = np.arange(128)
    k, q = i[:, None], i[None, :]
    cb[:, CB_ONES:CB_ONES + 128] = 1.0
    cb[:, CB_BONES:CB_BONES + 128] = (k // 64 == q // 64)
    cb[:, CB_ID:CB_ID + 128] = (k == q)
    cb[:, CB_OP0:CB_OP0 + 64] = 1.0
    cb[:, CB_OP1 + 64:CB_OP1 + 128] = 1.0
    mcur = np.where(q >= k, 0.0, NEG)
    mprv = np.where(k > q, 0.0, NEG)
    same = (k // 64 == q // 64)
    gm = ((k <= q) & same).astype(np.float32)
    tril = (k <= q).astype(np.float32)
    cb[:, CB_MCUR:CB_MCUR + 512] = np.tile(mcur, (1, 4))
    cb[:, CB_MPRV:CB_MPRV + 512] = np.tile(mprv, (1, 4))
    cb[:, CB_GM:CB_GM + 512] = np.tile(gm, (1, 4))
    cb[:, CB_TRIL:CB_TRIL + 512] = np.tile(tril, (1, 4))
    cf = np.zeros((128, CF_END), np.float32)
    cf[:, CF_UC:CF_UC + 128] = gm
    cf[:, CF_MGT:CF_MGT + 128] = ((k > q) & same)
    return cb, cf


LP_G1, LP_GM, LP_G2 = 0, 8, 16
LP_GQ, LP_GK, LP_GO = 24, 25, 26
LP_SINK = 27
LP_VN = 31
LP_BS = LP_VN + 256
LP_BG = LP_BS + 256
LP_WS = LP_BG + 256
LP_WGU = LP_WS + 512
LP_END = LP_WGU + 256


def _layer_params(inp, l):
    p = np.zeros((128, LP_END), np.float32)
    p[:, LP_G1:LP_G1 + 8] = _colT(inp["ffn1_norm"][l])
    p[:, LP_GM:LP_GM + 8] = _colT(inp["mix_norm"][l])
    p[:, LP_G2:LP_G2 + 8] = _colT(inp["ffn2_norm"][l])
    p[:, LP_GQ] = np.tile(inp["attn_q_norm"][l], 2)
    p[:, LP_GK] = np.tile(inp["attn_k_norm"][l], 2)
    p[:, LP_GO] = np.tile(inp["gla_out_norm"][l], 2)
    sk = inp["attn_sinks"][l]
    p[0:64, LP_SINK:LP_SINK + 4] = sk[None, 0:4]
    p[64:128, LP_SINK:LP_SINK + 4] = sk[None, 4:8]
    p[:, LP_VN:LP_VN + 256] = inp["sgu_v_norm"][l][None, :]
    bs = inp["sgu_b"][l]
    for j in range(2):
        for gg in range(2):
            p[gg * 64:(gg + 1) * 64, LP_BS + j * 128:LP_BS + (j + 1) * 128] = bs[2 * j + gg][None, :]
    p[:, LP_BG:LP_BG + 256] = inp["gla_b_gate"][l][None, :]
    ws = inp["sgu_w"][l]
    for g in range(4):
        p[:, LP_WS + g * 128:LP_WS + (g + 1) * 128] = ws[g].T
    p[0:16, LP_WGU:LP_WGU + 256] = inp["gla_w_gate_up"][l]
    return p


def build_program(T, TT=1024, depth=DEPTH, stages=("ffn1", "mix", "A", "B", "C", "ffn2"), dbg=99, dbg2=99):
    assert T % TT == 0 and TT % 512 == 0
    NT = T // TT
    NS = TT // 512
    nc = bass.Bass("TRN2", target_bir_lowering=False)
    xT = nc.dram_tensor("xT", [D_MODEL, T], F32, kind="ExternalInput").ap()
    wts = nc.dram_tensor("wts", [128, depth * LAYER_COLS], F32, kind="ExternalInput").ap()
    lpar = nc.dram_tensor("lpar", [128, depth * LP_END], F32, kind="ExternalInput").ap()
    cstb = nc.dram_tensor("cstb", [128, CB_END], F32, kind="ExternalInput").ap()
    cstf = nc.dram_tensor("cstf", [128, CF_END], F32, kind="ExternalInput").ap()
    yT = nc.dram_tensor("yT", [D_MODEL, T], F32, kind="ExternalOutput").ap()
    wsc = nc.dram_tensor("wsc", [128, depth * LAYER_COLS], BF16, kind="Internal").ap()

    with ExitStack() as st:
        P = Prog(nc, st)

        def tens(name, shape, dt):
            return st.enter_context(nc.sbuf_tensor(name, list(shape), dt))

        xres_t = tens("xres", [128, KC, TT], F32)
        xres = [[Buf(f"xres{k}_{s}", xres_t) for s in range(NS)] for k in range(KC)]
        hT_t = tens("hT", [128, KC, TT], BF16)
        hT = [[Buf(f"hT{k}_{s}", hT_t) for s in range(NS)] for k in range(KC)]
        act_raw = tens("act_raw", [128, max(FH * TT // 2, 5120)], F32)
        act_t = act_raw[:, 0:FH * TT // 2].bitcast(BF16).rearrange("p (f t) -> p f t", f=FH)
        actT = [[Buf(f"act{f}_{s}", act_t) for s in range(NS)] for f in range(FH)]
        act_all = [b for row in actT for b in row]

        NSTG, NWB = 3, 4
        stg = [P.sbuf(f"stg{i}", [128, UW], F32) for i in range(NSTG)]
        wbf_t = [tens(f"wbf{i}", [128, UW], BF16) for i in range(NWB)]
        wbf = [[Buf(f"wbf{i}_{j}", wbf_t[i]) for j in range(3)] for i in range(NWB)]
        late_t = list(wbf_t) + [stg[i].t[:, h * (UW // 2):(h + 1) * (UW // 2)].bitcast(BF16) for i in range(NSTG) for h in range(2)]
        late_b = list(wbf) + [[Buf(f"lw{i}_{h}", None)] for i in range(NSTG) for h in range(2)]
        sq_ring = Ring([P.sbuf(f"sq{i}", [128, 512], BF16) for i in range(4)])
        f32_ring = Ring([P.sbuf(f"fs{i}", [128, 512], F32) for i in range(4)])
        ps_ring = Ring([P.psum(f"ps{i}", [128, 512]) for i in range(6)])
        gla_o = [P.psum(f"glao{i}", [128, 512]) for i in range(2)]

        cb = P.sbuf("cb", [128, CB_END], BF16)
        cf = P.sbuf("cf", [128, CF_END], F32)
        lp = P.sbuf("lp", [128, depth * LP_END], F32)
        P.dma(cf[:], cstf[:, :], writes=[cf])
        P.dma(lp[:], lpar[:, :], writes=[lp])
        for i, c0 in enumerate(range(0, CB_END, UW)):
            w = min(UW, CB_END - c0)
            P.dma(stg[i][:, 0:w], cstb[:, c0:c0 + w], writes=[stg[i]])
            P.op("dve", "tensor_copy", out=cb[:, c0:c0 + w], in_=stg[i][:, 0:w], reads=[stg[i]], writes=[cb])
        ones_b = cb.t[:, CB_ONES:CB_ONES + 128]
        bones_b = cb.t[:, CB_BONES:CB_BONES + 128]
        ident_b = cb.t[:, CB_ID:CB_ID + 128]
        opad_b = [cb.t[:, CB_OP0:CB_OP0 + 128], cb.t[:, CB_OP1:CB_OP1 + 128]]
        mcur_b = cb.t[:, CB_MCUR:CB_MCUR + 512]
        mprv_b = cb.t[:, CB_MPRV:CB_MPRV + 512]
        gm_b = cb.t[:, CB_GM:CB_GM + 512]
        tril_b = cb.t[:, CB_TRIL:CB_TRIL + 512]
        uc_f = cf.t[:, CF_UC:CF_UC + 128]
        mgt_f = cf.t[:, CF_MGT:CF_MGT + 128]

        MIX = "mix" in stages
        if MIX:
            qhat = P.sbuf("qhat", [128, 4, 512], BF16)
            khat = [P.sbuf(f"khat{l}", [128, 2, 640], BF16) for l in range(depth)]
            Vp = [P.sbuf(f"Vp{l}", [128, 5, 2, 128], BF16) for l in range(depth)]
            uT = P.sbuf("uT", [128, 2, 512], BF16)
            vhp_t = tens("vhp", [128, 4, 4, 128], BF16)
            vhp = [Buf(f"vhp{b}", vhp_t) for b in range(4)]
            lrT = P.sbuf("lrT", [16, 512], BF16)
            qd = P.sbuf("qd", [128, 2, 512], BF16)
            ki = P.sbuf("ki", [128, 2, 512], BF16)
            kstp_t = tens("kstp", [128, 4, 4, 128], BF16)
            kstp = [Buf(f"kstp{b}", kstp_t) for b in range(4)]
            Vgp_t = tens("Vgp", [128, 4, 4, 128], BF16)
            Vgp = [Buf(f"Vgp{b}", Vgp_t) for b in range(4)]
            mix_t = tens("mixedT", [128, 8, 512], BF16)
            mixA, mixB, mixC = Buf("mixA", mix_t), Buf("mixB", mix_t), Buf("mixC", mix_t)
            pT_ring = Ring([P.sbuf(f"pT{i}", [128, 512], BF16) for i in range(6)])
            aT_ring = Ring([P.sbuf(f"aT{i}", [128, 512], BF16) for i in range(4)])
            small = P.sbuf("small", [128, 16], F32)
            Sst = [[[P.sbuf(f"S{l}_{hp}_{i}", [128, 128], F32) for i in range(2)] for hp in range(2)] for l in range(depth)]
            s_cur = [[0, 0] for _ in range(depth)]
            vn_t = tens("vn", [128, 8, 4], F32)
            vn = [Buf(f"vn{i}", vn_t) for i in range(8)]
            vn_i = [0]
            mhalf = P.sbuf("mhalf", [128, 2], F32)
            Sbw = [[P.sbuf(f"Sbw{hp}_{i}", [128, 128], BF16) for i in range(9)] for hp in range(2)]
            esink = [P.sbuf(f"esink{l}", [128, 512], F32) for l in range(depth)]
            wcT = [P.sbuf(f"wcT{l}", [128, 512], BF16) for l in range(depth)]
            wgu = [P.sbuf(f"wgu{l}", [16, 256], BF16) for l in range(depth)]
            o = 0
            sp_t = act_raw[:, o:o + 1024].rearrange("p (b c) -> p b c", b=4); o += 1024
            eneg_t = act_raw[:, o:o + 1024].rearrange("p (j t) -> p j t", j=2); o += 1024
            epos_t = act_raw[:, o:o + 1024].rearrange("p (j t) -> p j t", j=2); o += 1024
            etm_t = act_raw[:, o:o + 1024].rearrange("p (b c) -> p b c", b=4); o += 1024
            sgate_t = act_raw[:, o:o + 1024].rearrange("p (j t) -> p j t", j=2); o += 1024
            spB = [Buf(f"sp{b}", sp_t) for b in range(4)]
            enegB, eposB = Buf("eneg", eneg_t), Buf("epos", epos_t)
            etmB = [Buf(f"etm{b}", etm_t) for b in range(4)]
            sgateB = Buf("sgate", sgate_t)
            alias_bufs = spB + [enegB, eposB] + etmB + [sgateB]

            for t_ in (vhp_t, kstp_t, Vgp_t):
                P.op("pool", "memset", t_[:], 0.0)
            P.op("pool", "memset", mhalf[:], -0.5, writes=[mhalf])
            for l in range(depth):
                P.op("pool", "memset", khat[l][:], 0.0, writes=[khat[l]])
                P.op("pool", "memset", Vp[l][:], 0.0, writes=[Vp[l]])
                for hp in range(2):
                    P.op("pool", "memset", Sst[l][hp][0][:], 0.0, writes=[Sst[l][hp][0]])
                lo = l * LP_END
                P.op("act", "activation", out=small[:, 0:4], in_=lp[:, lo + LP_SINK:lo + LP_SINK + 4], func=AF.Exp,
                     reads=[lp], writes=[small])
                for c in range(4):
                    P.op("dve", "tensor_scalar", out=esink[l][:, c * 128:(c + 1) * 128], in0=uc_f, scalar1=0.0,
                         scalar2=small[:, c:c + 1], op0=ALU.mult, op1=ALU.add, reads=[cf, small], writes=[esink[l]])
                P.op("dve", "tensor_tensor", out=wcT[l][:], in0=lp[:, lo + LP_WS:lo + LP_WS + 512], in1=tril_b, op=ALU.mult,
                     reads=[lp, cb], writes=[wcT[l]])
                P.op("dve", "tensor_copy", out=wgu[l][:], in_=lp[0:16, lo + LP_WGU:lo + LP_WGU + 256], reads=[lp], writes=[wgu[l]])
            for b in range(4):
                for B_ in (vhp[b], kstp[b], Vgp[b]):
                    B_.last_w = ("pool", P.count["pool"])

        ws = {"n": 0, "tile": 0, "pend": [], "stored": {}, "late": 0}

        class WB:
            def __init__(self, t, bufs):
                self.t, self.bufs = t, bufs

        def wflush(keep):
            while len(ws["pend"]) > keep:
                col0, width, i = ws["pend"].pop(0)
                ws["stored"][col0] = P.dma(wsc[:, col0:col0 + width], wbf_t[i % NWB][:, 0:width], reads=wbf[i % NWB],
                                           sem_buf=wbf[i % NWB][0])

        def tile_units():
            seq = []

            def ffn_u(c):
                for hf in range(2):
                    for f in range(FH):
                        seq.append((c, W1)); c += W1
                    for dc in range(KC):
                        seq.append((c, W2)); c += W2
            for l in range(depth):
                base = l * LAYER_COLS
                if "ffn1" in stages:
                    ffn_u(base)
                if "mix" in stages:
                    for s_ in range(NS):
                        for u in range(N_FMU + N_TMU + N_OUTU):
                            seq.append((base + FFN_COLS + u * UW, UW))
                if "ffn2" in stages:
                    ffn_u(base + FFN_COLS + MIX_COLS)
            return seq

        useq = tile_units()
        LA = 1
        ws["pos"] = 0
        ws["issued"] = 0
        ws["slot"] = {}

        def wissue(p):
            col0, width = useq[p]
            i = ws["n"]
            ws["n"] += 1
            wt, wb = wbf_t[i % NWB], wbf[i % NWB]
            sg = stg[i % NSTG]
            P.dma(sg[:, 0:width], wts[:, col0:col0 + width], writes=[sg])
            c1, c2 = (width // 4) // 64 * 64, (5 * width // 8) // 64 * 64
            P.op("pool", "tensor_copy", out=wt[:, 0:c1], in_=sg[:, 0:c1], reads=[sg], writes=[wb[0]])
            P.op("act", "activation", out=wt[:, c1:c2], in_=sg[:, c1:c2], func=AF.Copy, reads=[sg], writes=[wb[1]])
            P.op("dve", "tensor_copy", out=wt[:, c2:width], in_=sg[:, c2:width], reads=[sg], writes=[wb[2]])
            ws["pend"].append((col0, width, i))
            wflush(2)
            ws["slot"][p] = WB(wt, wb)

        def wnext(col0, width):
            if ws["tile"] == 0:
                p = ws["pos"]
                assert useq[p] == (col0, width), (p, useq[p], col0, width)
                while ws["issued"] < min(p + 1 + LA, len(useq)):
                    wissue(ws["issued"])
                    ws["issued"] += 1
                ws["pos"] += 1
                return ws["slot"].pop(p)
            j = ws["late"] % len(late_t)
            ws["late"] += 1
            wt, wb = late_t[j], late_b[j]
            P.wait_event("sp", ws["stored"][col0])
            P.dma(wt[:, 0:width], wsc[:, col0:col0 + width], writes=wb, sem_buf=wb[-1])
            return WB(wt, wb)

        def mm(ps_ap, lhsT, rhs, first, last, reads, writes, inc_all=False):
            P.op("pe", "matmul", ps_ap, lhsT=lhsT, rhs=rhs, start=first, stop=last,
                 reads=reads, writes=writes, inc=(last or inc_all), skip_self=True)

        def alias(dst, src):
            merged = {}
            for b in src:
                evs = list(b.readers.items())
                if b.last_w is not None:
                    evs.append(("w_" + b.last_w[0], b.last_w))
                for k, ev in evs:
                    if k not in merged or merged[k][1] < ev[1] or merged[k][0] != ev[0]:
                        if k in merged and merged[k][0] != ev[0]:
                            k = k + "_" + ev[0]
                        if k not in merged or merged[k][1] < ev[1]:
                            merged[k] = ev
            for b in dst:
                for k, ev in merged.items():
                    if k not in b.readers or b.readers[k][1] < ev[1]:
                        b.readers[k] = ev

        def rstd_from(pss, width, dim):
            rs = f32_ring.get()
            P.op("act", "activation", out=rs[:, 0:width], in_=pss[:, 0:width], func=AF.Ln, bias=RMS_EPS, scale=1.0 / dim,
                 reads=[pss], writes=[rs])
            P.op("act", "activation", out=rs[:, 0:width], in_=rs[:, 0:width], func=AF.Exp, scale=-0.5, reads=[rs], writes=[rs])
            return rs

        def norm_gen(lcol, s):
            tsl = slice(s * 512, (s + 1) * 512)
            pss = ps_ring.get()
            sqs = {}

            def square(k):
                sqs[k] = sq_ring.get()
                P.op("act", "activation", out=sqs[k][:], in_=xres_t[:, k, tsl], func=AF.Square, reads=[xres[k][s]], writes=[sqs[k]])

            def accum(k):
                mm(pss[:], ones_b, sqs[k][:], k == 0, k == KC - 1, [sqs[k], cb], [pss], inc_all=True)
                Ring.rel(sqs[k])

            for k in range(4):
                square(k)
            yield
            for k in range(4):
                accum(k)
            for k in range(4, 8):
                square(k)
            yield
            for k in range(4, 8):
                accum(k)
            rs = rstd_from(pss, 512, D_MODEL)
            Ring.rel(pss)
            for k in range(KC):
                P.op("dve", "scalar_tensor_tensor", out=hT_t[:, k, tsl], in0=xres_t[:, k, tsl],
                     scalar=lp[:, lcol + k:lcol + k + 1], in1=rs[:], op0=ALU.mult, op1=ALU.mult,
                     reads=[xres[k][s], rs, lp], writes=[hT[k][s]])
            Ring.rel(rs)

        def norm_to_h(lcol, only=None):
            for s in (range(NS) if only is None else [only]):
                for _ in norm_gen(lcol, s):
                    pass

        def ffn(wcol, lcol, normed=False, after_s=None):
            if not normed:
                norm_to_h(lcol)
            normed_late = False
            c = wcol
            for hf in range(2):
                def gate_up(wb, f, s):
                    wv = wb.t[:, 0:W1].rearrange("p (m k c) -> p m k c", m=2, k=KC)
                    tsl = slice(s * 512, (s + 1) * 512)
                    pg = ps_ring.get()
                    pu = ps_ring.get()
                    for m, pp in ((0, pg), (1, pu)):
                        for k in range(KC):
                            mm(pp[:], wv[:, m, k, :], hT_t[:, k, tsl], k == 0, k == KC - 1, wb.bufs + [hT[k][s]], [pp])
                    sg = f32_ring.get()
                    P.op("act", "activation", out=sg[:], in_=pg[:], func=AF.Silu, reads=[pg], writes=[sg])
                    P.op("dve", "tensor_tensor", out=act_t[:, f, tsl], in0=pu[:], in1=sg[:], op=ALU.mult,
                         reads=[pu, sg], writes=[actT[f][s]])
                    Ring.rel(pg, pu, sg)

                NHEAD = 3 if (hf == 0 and NS > 1 and not normed_late) else 0
                head = []
                for f in range(NHEAD):
                    head.append(wnext(c, W1))
                    c += W1
                for s in range(NS):
                    for f in range(NHEAD):
                        gate_up(head[f], f, s)
                for f in range(NHEAD, FH):
                    wb = wnext(c, W1)
                    c += W1
                    for s in range(NS):
                        gate_up(wb, f, s)
                def down(wb, dc, s):
                    wv = wb.t[:, 0:W2].rearrange("p (k c) -> p k c", k=FH)
                    tsl = slice(s * 512, (s + 1) * 512)
                    py = ps_ring.get()
                    for f in range(FH):
                        mm(py[:], wv[:, f, :], act_t[:, f, tsl], f == 0, f == FH - 1, wb.bufs + [actT[f][s]], [py])
                    P.op("dve", "scalar_tensor_tensor", out=xres_t[:, dc, tsl], in0=py[:], scalar=0.5,
                         in1=xres_t[:, dc, tsl], op0=ALU.mult, op1=ALU.add,
                         reads=[py, xres[dc][s]], writes=[xres[dc][s]])
                    Ring.rel(py)

                if hf == 1 and ws["tile"] > 0 and NS > 1:
                    units = []
                    for dc in range(KC):
                        units.append(wnext(c, W2))
                        c += W2
                    gen = None
                    for s in range(NS):
                        for dc in range(KC):
                            down(units[dc], dc, s)
                            if gen is not None and dc >= 1:
                                next(gen, None)
                        if gen is not None:
                            for _ in gen:
                                pass
                        gen = after_s(s) if after_s is not None else None
                    if gen is not None:
                        for _ in gen:
                            pass
                else:
                    for dc in range(KC):
                        wb = wnext(c, W2)
                        c += W2
                        for s in range(NS):
                            down(wb, dc, s)
                    if hf == 1 and after_s is not None:
                        for s in range(NS):
                            for _ in after_s(s):
                                pass

        def mixer_sub(wcol, l, s, first):
            lo = l * LP_END
            t0 = s * 512
            tsl = slice(t0, t0 + 512)
            c = wcol
            kh, vp = khat[l], Vp[l]
            doA, doB, doC = "A" in stages, "B" in stages, "C" in stages
            if not first:
                P.op("pool", "tensor_copy", out=kh[:, :, 0:128], in_=kh[:, :, 512:640], reads=[kh], writes=[kh])
                P.op("pool", "tensor_copy", out=vp[:, 0, :, :], in_=vp[:, 4, :, :], reads=[vp], writes=[vp])

            def diag2(t3, j):
                flat = t3.rearrange("p h c -> p (h c)")
                if j == 0:
                    return flat[:, 0:384].rearrange("p (a c) -> p a c", c=192)[:, :, 0:64]
                return flat[:, 128:512].rearrange("p (a c) -> p a c", c=192)[:, :, 128:192]

            def proj_fm(wb, j, M=128):
                wv = wb.t[:, 0:UW].rearrange("p (j k c) -> p j k c", j=2, k=KC)
                ps = ps_ring.get()
                for k in range(KC):
                    mm(ps[0:M, :], wv[:, j, k, 0:M], hT_t[:, k, tsl], k == 0, k == KC - 1, wb.bufs + [hT[k][s]], [ps])
                return ps

            def qk_part1(ps):
                sq = sq_ring.get()
                P.op("act", "activation", out=sq[:], in_=ps[:], func=AF.Square, reads=[ps], writes=[sq])
                return sq

            def qk_part2(ps, sq, is_k, cq):
                ps2 = ps_ring.get()
                mm(ps2[:], bones_b, sq[:], True, True, [sq, cb], [ps2])
                Ring.rel(sq)
                rs = rstd_from(ps2, 512, 64)
                Ring.rel(ps2)
                if is_k:
                    for g in range(2):
                        r = slice(g * 64, (g + 1) * 64)
                        P.op("dve", "scalar_tensor_tensor", out=kh[r, g, 128:640], in0=ps[r, :],
                             scalar=lp[r, lo + LP_GK:lo + LP_GK + 1], in1=rs[r, :], op0=ALU.mult, op1=ALU.mult,
                             reads=[ps, rs, lp], writes=[kh])
                else:
                    P.op("dve", "scalar_tensor_tensor", out=qhat[:, cq, :], in0=ps[:],
                         scalar=lp[:, lo + LP_GQ:lo + LP_GQ + 1], in1=rs[:], op0=ALU.mult, op1=ALU.mult,
                         reads=[ps, rs, lp], writes=[qhat])
                Ring.rel(ps, rs)

            def qk_norm(ps, is_k, cq):
                qk_part2(ps, qk_part1(ps), is_k, cq)

            if dbg < 1:
                return
            wb = wnext(c, UW); c += UW
            ps = proj_fm(wb, 0, M=16)
            P.op("act", "activation", out=lrT[:, :], in_=ps[0:16, :], func=AF.Copy, reads=[ps], writes=[lrT])
            Ring.rel(ps)
            ps = proj_fm(wb, 1)
            if doA:
                qk_norm(ps, True, 0)
            else:
                Ring.rel(ps)
            if dbg < 2:
                return
            if doC:
                pls, zs = [], []
                for b in range(4):
                    pl = ps_ring.get()
                    mm(pl[:, 0:256], lrT[0:16, b * 128:(b + 1) * 128], wgu[l][:, :], True, True, [lrT, wgu[l]], [pl])
                    pls.append(pl)
                for b in range(4):
                    z = f32_ring.get()
                    P.op("dve", "tensor_tensor", out=z[:, 0:256], in0=pls[b][:, 0:256], in1=lp[:, lo + LP_BG:lo + LP_BG + 256],
                         op=ALU.add, reads=[pls[b], lp], writes=[z])
                    Ring.rel(pls[b])
                    zs.append(z)
                for b in range(4):
                    P.op("act", "activation", out=zs[b][:, 0:256], in_=zs[b][:, 0:256], func=AF.Exp, scale=-1.0, reads=[zs[b]], writes=[zs[b]])
                for b in range(4):
                    P.op("act", "activation", out=sp_t[:, b, :], in_=zs[b][:, 0:256], func=AF.Ln, bias=1.0, reads=[zs[b]], writes=[spB[b]])
                    Ring.rel(zs[b])
            qpend = [None]
            for u in range(1, N_FMU):
                if u == 3:
                    if qpend[0] is not None:
                        qk_part2(*qpend[0])
                        qpend[0] = None
                    if doC:
                        pcs = []
                        for b in range(4):
                            pc = ps_ring.get()
                            for j in range(2):
                                mm(pc[:, j * 128:(j + 1) * 128], sp_t[:, b, j * 128:(j + 1) * 128], uc_f, True, True, [spB[b], cf], [pc], inc_all=True)
                            mm(pc[:, 256:512], mgt_f, sp_t[:, b, :], True, True, [spB[b], cf], [pc], inc_all=True)
                            pcs.append(pc)
                        for b in range(4):
                            bsl = slice(b * 128, (b + 1) * 128)
                            pcv = pcs[b].t[:, 0:256].rearrange("p (j t) -> p j t", j=2)
                            P.op("act", "activation", out=eneg_t[:, :, bsl], in_=pcv, func=AF.Exp, scale=-1.0 / 16, reads=[pcs[b]], writes=[enegB])
                            P.op("act", "activation", out=epos_t[:, :, bsl], in_=pcv, func=AF.Exp, scale=1.0 / 16, reads=[pcs[b]], writes=[eposB])
                            P.op("act", "activation", out=etm_t[:, b, :], in_=pcs[b][:, 256:512], func=AF.Exp, scale=-1.0 / 16, reads=[pcs[b]], writes=[etmB[b]])
                            Ring.rel(pcs[b])
                wb = wnext(c, UW); c += UW
                for j in range(2):
                    ci = 2 * u + j
                    need = (ci <= 5 and doA) or (ci in (6, 7) and doB) or (ci >= 8 and doC)
                    if not need:
                        continue
                    ps = proj_fm(wb, j)
                    if ci <= 5:
                        cur = (ps, qk_part1(ps), False, ci - 2)
                        if qpend[0] is not None:
                            qk_part2(*qpend[0])
                        qpend[0] = cur
                        continue
                    if qpend[0] is not None:
                        qk_part2(*qpend[0])
                        qpend[0] = None
                    if ci <= 7:
                        P.op("act", "activation", out=uT[:, ci - 6, :], in_=ps[:], func=AF.Gelu_apprx_tanh, reads=[ps], writes=[uT])
                        Ring.rel(ps)
                    elif ci <= 9:
                        P.op("dve", "scalar_tensor_tensor", out=qd[:, ci - 8, :], in0=ps[:], scalar=0.125, in1=eneg_t[:, ci - 8, :],
                             op0=ALU.mult, op1=ALU.mult, reads=[ps, enegB], writes=[qd])
                        Ring.rel(ps)
                    elif ci <= 11:
                        P.op("dve", "tensor_tensor", out=ki[:, ci - 10, :], in0=ps[:], in1=epos_t[:, ci - 10, :], op=ALU.mult,
                             reads=[ps, eposB], writes=[ki])
                        Ring.rel(ps)
                    else:
                        P.op("act", "activation", out=sgate_t[:, ci - 12, :], in_=ps[:], func=AF.Silu, reads=[ps], writes=[sgateB])
                        Ring.rel(ps)
            if qpend[0] is not None:
                qk_part2(*qpend[0])
                qpend[0] = None
            if dbg < 3:
                return
            for u in range(N_TMU // 2):
                wbs = [wnext(c, UW), wnext(c + UW, UW)]
                c += 2 * UW
                wvs = [w_.t[:, 0:UW].rearrange("p (k c) -> p k c", k=4) for w_ in wbs]
                for b in range(4):
                    pieces = [pi for pi in range(4 * u, 4 * u + 4) if pi < 7 and pi in _PCS and ((pi == 0 and doA) or (pi in (1, 2) and doB) or (pi >= 3 and doC))]
                    if not pieces:
                        continue
                    ps = ps_ring.get()
                    for k in range(KC):
                        mm(ps[:, :], hT_t[:, k, t0 + b * 128:t0 + (b + 1) * 128], wvs[k // 4][:, k % 4, :], k == 0, k == KC - 1,
                           wbs[k // 4].bufs + [hT[k][s]], [ps])
                    pieces = sorted(pieces, key=lambda q_: 0 if q_ >= 3 else 1)
                    dve_first = [kstp[b]] if any(q_ >= 3 for q_ in pieces) else []
                    for pi in pieces:
                        co = (pi % 4) * 128
                        if pi == 0:
                            vflat = vp.t[:, :, :, :].rearrange("p b g c -> p (b g c)")
                            vdst = vflat[:, (1 + b) * 256 - 128:(1 + b) * 256 + 256].rearrange("p (a c) -> p a c", c=192)[:, :, 128:192]
                            P.op("act", "activation", out=vdst, in_=ps[:, co:co + 128].rearrange("p (a c) -> p a c", a=2),
                                 func=AF.Copy, reads=[ps] + dve_first, writes=[vp])
                        elif pi <= 2:
                            j = pi - 1
                            vg = f32_ring.get()
                            P.op("act", "activation", out=vg[:, 0:128], in_=ps[:, co:co + 128], func=AF.Gelu_apprx_tanh, reads=[ps] + dve_first, writes=[vg])
                            vs = vn_i[0] % 8
                            vn_i[0] += 1
                            for gg in range(2):
                                P.op("act", "activation", out=vg[:, 256 + gg * 64:256 + (gg + 1) * 64], in_=vg[:, gg * 64:(gg + 1) * 64], func=AF.Square,
                                     accum_out=vn_t[:, vs, gg:gg + 1], reads=[vg], writes=[vg, vn[vs]])
                            P.op("pool", "tensor_scalar", out=vn_t[:, vs, 2:4], in0=vn_t[:, vs, 0:2], scalar1=1.0 / 64, scalar2=RMS_EPS,
                                 op0=ALU.mult, op1=ALU.add, reads=[vn[vs]], writes=[vn[vs]])
                            P.op("pool", "tensor_tensor", out=vn_t[:, vs, 2:4], in0=vn_t[:, vs, 2:4], in1=mhalf[:, :], op=ALU.pow,
                                 reads=[vn[vs], mhalf], writes=[vn[vs]])
                            for gg in range(2):
                                g4 = 2 * j + gg
                                P.op("dve", "scalar_tensor_tensor", out=vhp_t[:, b, g4, gg * 64:(gg + 1) * 64], in0=vg[:, gg * 64:(gg + 1) * 64],
                                     scalar=vn_t[:, vs, 2 + gg:3 + gg], in1=lp[:, lo + LP_VN + g4 * 64:lo + LP_VN + (g4 + 1) * 64],
                                     op0=ALU.mult, op1=ALU.mult, reads=[vg, vn[vs], lp], writes=[vhp[b]])
                            Ring.rel(vg)
                        elif pi <= 4:
                            j = pi - 3
                            P.op("dve", "tensor_tensor", out=diag2(kstp_t[:, b, :, :], j), in0=ps[:, co:co + 128].rearrange("p (a c) -> p a c", a=2),
                                 in1=etm_t[:, b, j * 128:(j + 1) * 128].rearrange("p (a c) -> p a c", a=2), op=ALU.mult,
                                 reads=[ps, etmB[b]], writes=[kstp[b]])
                        else:
                            j = pi - 5
                            P.op("dve", "tensor_copy", out=diag2(Vgp_t[:, b, :, :], j), in_=ps[:, co:co + 128].rearrange("p (a c) -> p a c", a=2),
                                 reads=[ps], writes=[Vgp[b]])
                    Ring.rel(ps)
            if dbg < 4:
                return
            gsq = [None, None]

            def gla_norm_p1():
                for hp in range(2):
                    gsq[hp] = sq_ring.get()
                    P.op("act", "activation", out=gsq[hp][:], in_=gla_o[hp][:], func=AF.Square, reads=[gla_o[hp]], writes=[gsq[hp]])

            def gla_norm_p2():
                for hp in range(2):
                    po = gla_o[hp]
                    ps2 = ps_ring.get()
                    mm(ps2[:], bones_b, gsq[hp][:], True, True, [gsq[hp], cb], [ps2])
                    Ring.rel(gsq[hp])
                    rs = rstd_from(ps2, 512, 64)
                    Ring.rel(ps2)
                    tmp = f32_ring.get()
                    P.op("dve", "scalar_tensor_tensor", out=tmp[:], in0=po[:], scalar=lp[:, lo + LP_GO:lo + LP_GO + 1], in1=rs[:],
                         op0=ALU.mult, op1=ALU.mult, reads=[po, rs, lp], writes=[tmp])
                    P.op("dve", "tensor_tensor", out=mix_t[:, 6 + hp, :], in0=tmp[:], in1=sgate_t[:, hp, :], op=ALU.mult,
                         reads=[tmp, sgateB], writes=[mixC])
                    Ring.rel(rs, tmp)

            aTs = [None] * 4
            if doC:
                for hp in range(2):
                    sc_ = Sst[l][hp][s_cur[l][hp]]
                    P.op("pool", "tensor_copy", out=Sbw[hp][0][:, :], in_=sc_[:, :], reads=[sc_], writes=[Sbw[hp][0]])
                for b in range(4):
                    bsl = slice(b * 128, (b + 1) * 128)
                    pAs = [ps_ring.get(), ps_ring.get()]
                    aT = aT_ring.get()
                    aTs[b] = aT
                    aT4 = aT.t[:, :].rearrange("p (j h t) -> p j h t", j=2, h=2)
                    for hh in range(2):
                        r = slice(hh * 64, (hh + 1) * 64)
                        for j in range(2):
                            mm(pAs[hh][:, j * 128:(j + 1) * 128], ki[r, j, bsl], qd[r, j, bsl], True, True, [ki, qd], [pAs[hh]], inc_all=True)
                        P.op("dve", "tensor_tensor", out=aT4[:, :, hh, :], in0=pAs[hh].t[:, 0:256].rearrange("p (j t) -> p j t", j=2),
                             in1=gm_b[:, 0:256].rearrange("p (j t) -> p j t", j=2), op=ALU.mult, reads=[pAs[hh], cb], writes=[aT])
                    Ring.rel(*pAs)
                    pdl = [gla_o[0], gla_o[1]] if b % 2 == 1 else [ps_ring.get(), ps_ring.get()]
                    for cc in range(2):
                        r = slice(cc * 64, (cc + 1) * 64)
                        for hp in range(2):
                            osl = slice(hp * 128, (hp + 1) * 128)
                            mm(pdl[cc][:, osl], kstp_t[r, b, 2 * hp, :], Vgp_t[r, b, 2 * hp, :], True, False, [kstp[b], Vgp[b]], [pdl[cc]], inc_all=True)
                            mm(pdl[cc][:, osl], kstp_t[r, b, 2 * hp + 1, :], Vgp_t[r, b, 2 * hp + 1, :], False, True, [kstp[b], Vgp[b]], [pdl[cc]], inc_all=True)
                    for cc in range(2):
                        tk = b * 128 + cc * 64
                        for hp in range(2):
                            so = Sst[l][hp][s_cur[l][hp]]
                            s_cur[l][hp] ^= 1
                            sn = Sst[l][hp][s_cur[l][hp]]
                            P.op("dve", "scalar_tensor_tensor", out=sn[:, :], in0=so[:, :], scalar=eneg_t[:, hp, tk + 63:tk + 64],
                                 in1=pdl[cc][:, hp * 128:(hp + 1) * 128], op0=ALU.mult, op1=ALU.add, reads=[so, enegB, pdl[cc]], writes=[sn])
                            sbn = Sbw[hp][2 * b + cc + 1]
                            P.op("pool", "tensor_copy", out=sbn[:, :], in_=sn[:, :], reads=[sn], writes=[sbn])
                    if b % 2 == 0:
                        Ring.rel(*pdl)

            for b in range(4):
                bsl = slice(b * 128, (b + 1) * 128)
                if doB:
                    pz = ps_ring.get()
                    for j in range(2):
                        for gg in range(2):
                            g4 = 2 * j + gg
                            mm(pz[:, j * 128:(j + 1) * 128], vhp_t[:, b, g4, :], wcT[l][:, g4 * 128:(g4 + 1) * 128], gg == 0, gg == 1,
                               [vhp[b], wcT[l]], [pz], inc_all=True)
                    tz = f32_ring.get()
                    P.op("dve", "tensor_tensor", out=tz[:, 0:256], in0=pz[:, 0:256], in1=lp[:, lo + LP_BS:lo + LP_BS + 256], op=ALU.add,
                         reads=[pz, lp], writes=[tz])
                    P.op("dve", "tensor_tensor", out=mix_t[:, 4:6, bsl], in0=tz.t[:, 0:256].rearrange("p (j t) -> p j t", j=2),
                         in1=uT[:, :, bsl], op=ALU.mult, reads=[tz, uT], writes=[mixB])
                    Ring.rel(pz, tz)

            def gla_out(b):
                bsl = slice(b * 128, (b + 1) * 128)
                aT = aTs[b]
                for hp in range(2):
                    po = gla_o[hp]
                    mm(po[:, bsl], Vgp_t[:, b, 2 * hp, :], aT[:, (2 * hp) * 128:(2 * hp + 1) * 128], True, False, [Vgp[b], aT], [po], inc_all=True)
                    mm(po[:, bsl], Vgp_t[:, b, 2 * hp + 1, :], aT[:, (2 * hp + 1) * 128:(2 * hp + 2) * 128], False, False, [Vgp[b], aT], [po], inc_all=True)
                    for cc in range(2):
                        tk = b * 128 + cc * 64
                        sbc = Sbw[hp][2 * b + cc]
                        mm(po[:, tk:tk + 64], sbc[:, :], qd[:, hp, tk:tk + 64], False, cc == 1, [sbc, qd], [po], inc_all=True)
                Ring.rel(aT)

            def attn_block(b):
                bsl = slice(b * 128, (b + 1) * 128)
                if True:
                    pts = []
                    for g in range(2):
                        for cur in (0, 1):
                            if first and b == 0 and not cur:
                                continue
                            pS = ps_ring.get()
                            kb = b + cur
                            mm(pS[:], kh[:, g, kb * 128:(kb + 1) * 128], qhat[:, :, bsl], True, False, [kh, qhat], [pS], inc_all=True)
                            mm(pS[:], ident_b, mcur_b if cur else mprv_b, False, True, [cb], [pS])
                            pt = pT_ring.get()
                            P.op("act", "activation", out=pt[:], in_=pS[:], func=AF.Exp, scale=0.125, reads=[pS], writes=[pt])
                            Ring.rel(pS)
                            pts.append((g, kb, pt))
                    if dbg < 6:
                        Ring.rel(*[p_[2] for p_ in pts])
                        return
                    po = ps_ring.get()
                    pd = ps_ring.get()
                    for i, (g, kb, pt) in enumerate(pts):
                        mm(po[:], vp[:, kb, g, :], pt[:], i == 0, i == len(pts) - 1, [vp, pt], [po], inc_all=True)
                    for i, (g, kb, pt) in enumerate(pts):
                        mm(pd[:], opad_b[g], pt[:], i == 0, i == len(pts) - 1, [cb, pt], [pd], inc_all=True)
                    Ring.rel(*[p_[2] for p_ in pts])
                    if dbg < 7:
                        Ring.rel(po, pd)
                        return
                    dt = f32_ring.get()
                    P.op("dve", "tensor_tensor", out=dt[:], in0=pd[:], in1=esink[l][:], op=ALU.add, reads=[pd, esink[l]], writes=[dt])
                    P.op("dve", "reciprocal", out=dt[:], in_=dt[:], reads=[dt], writes=[dt])
                    P.op("dve", "tensor_tensor", out=mix_t[:, 0:4, bsl], in0=po.t[:, :].rearrange("p (c q) -> p c q", c=4),
                         in1=dt.t[:, :].rearrange("p (c q) -> p c q", c=4), op=ALU.mult, reads=[po, dt], writes=[mixA])
                    Ring.rel(po, pd, dt)

            for b in range(4):
                if doA and b < 3:
                    attn_block(b)
                if doC:
                    gla_out(b)
            if doC:
                gla_norm_p1()
            if doA:
                attn_block(3)
            if doC:
                gla_norm_p2()
            if dbg < 8:
                return
            for u in range(N_OUTU):
                wb = wnext(c, UW); c += UW
                wv = wb.t[:, 0:UW].rearrange("p (j k c) -> p j k c", j=2, k=KC)
                for j in range(2):
                    dc = 2 * u + j
                    ks = [k for k in range(KC) if (k < 4 and doA) or (k in (4, 5) and doB) or (k >= 6 and doC)]
                    if not ks:
                        continue
                    ps = ps_ring.get()
                    for i, k in enumerate(ks):
                        mm(ps[:], wv[:, j, k, :], mix_t[:, k, :], i == 0, i == len(ks) - 1,
                           wb.bufs + [mixA if k < 4 else (mixB if k < 6 else mixC)], [ps])
                    P.op("dve", "tensor_tensor", out=xres_t[:, dc, tsl], in0=ps[:], in1=xres_t[:, dc, tsl], op=ALU.add,
                         reads=[ps, xres[dc][s]], writes=[xres[dc][s]])
                    Ring.rel(ps)

        def mixer(wcol, l, it, normed=False):
            if not normed:
                norm_to_h(l * LP_END + LP_GM)
            alias(alias_bufs, act_all)
            for s in range(NS):
                mixer_sub(wcol, l, s, first=(it == 0 and s == 0))
                if "ffn2" in stages and s < NS - 1:
                    norm_to_h(l * LP_END + LP_G2, only=s)
            if "ffn2" in stages:
                norm_to_h(l * LP_END + LP_G2, only=NS - 1)
            alias(act_all, alias_bufs)

        xT_v = xT.rearrange("(k p) t -> p k t", p=128)
        yT_v = yT.rearrange("(k p) t -> p k t", p=128)
        out_evs = []
        F1, F2 = "ffn1" in stages, "ffn2" in stages
        for it in range(NT):
            t0 = it * TT
            ws["tile"] = it
            for s_ in range(NS):
                for k in range(KC):
                    P.dma(xres_t[:, k, s_ * 512:(s_ + 1) * 512], xT_v[:, k, t0 + s_ * 512:t0 + (s_ + 1) * 512],
                          writes=[xres[k][s_]])

            def store(s_, t0=t0):
                for k in range(KC):
                    out_evs.append(P.dma(yT_v[:, k, t0 + s_ * 512:t0 + (s_ + 1) * 512], xres_t[:, k, s_ * 512:(s_ + 1) * 512],
                                         reads=[xres[k][s_]]))
                return iter(())

            stored = False
            for l in range(depth):
                base = l * LAYER_COLS
                lo_ = l * LP_END
                if F1:
                    nxt = (lambda s_, lo_=lo_: norm_gen(lo_ + LP_GM, s_)) if MIX else None
                    ffn(base, lo_ + LP_G1, normed=(l > 0 and F2), after_s=nxt)
                if MIX:
                    mixer(base + FFN_COLS, l, it, normed=F1)
                if F2:
                    if l + 1 < depth and F1:
                        nxt = (lambda s_, lo2=(l + 1) * LP_END: norm_gen(lo2 + LP_G1, s_))
                    elif l + 1 == depth:
                        nxt = store
                        stored = True
                    else:
                        nxt = None
                    ffn(base + FFN_COLS + MIX_COLS, lo_ + LP_G2, normed=MIX, after_s=nxt)
            wflush(0)
            if it == 0 and NT > 1:
                alias([b_ for bl in late_b[NWB:] for b_ in bl], stg)
            if not stored:
                for s_ in range(NS):
                    store(s_)
        for ev in out_evs:
            P.wait_event("sp", ev)
        P.emit()
    return nc


_NC_CACHE = {}


def kernel(**inp):
    inp = {k: np.asarray(v) for k, v in inp.items()}
    x = inp["x"]
    B, T, _ = x.shape
    key = (T,)
    if key not in _NC_CACHE:
        _NC_CACHE[key] = build_program(T)
    nc = _NC_CACHE[key]
    wts = np.concatenate([_pack_layer_weights(inp, l) for l in range(DEPTH)], axis=1)
    lpar = np.concatenate([_layer_params(inp, l) for l in range(DEPTH)], axis=1)
    cb, cf = _consts()
    in_maps = [{"xT": np.ascontiguousarray(x[b].T), "wts": wts, "lpar": lpar, "cstb": cb, "cstf": cf} for b in range(B)]
    res = run_bass_kernel_spmd(nc, in_maps, core_ids=list(range(B)))
    return np.stack([np.ascontiguousarray(res.results[b]["yT"].T) for b in range(B)], axis=0)
```

```python
import numpy as np
from contextlib import ExitStack
import concourse.bass as bass
import concourse.mybir as mybir
from concourse.bass_utils import run_bass_kernel_spmd

F32 = mybir.dt.float32
BF16 = mybir.dt.bfloat16
AF = mybir.ActivationFunctionType
ALU = mybir.AluOpType
AX = mybir.AxisListType

D_MODEL = 1024
DEPTH = 2
D_FF = 2816
KC = D_MODEL // 128
FC = D_FF // 128
FH = FC // 2
RMS_EPS = 1e-6
UW = 2048
W1 = 2 * KC * 128
W2 = FH * 128
N_FM = 14
N_FMU = N_FM // 2
N_TMU = 4
N_OUTU = 4
FFN_COLS = 2 * (FH * W1 + KC * W2)
MIX_COLS = (N_FMU + N_TMU + N_OUTU) * UW
LAYER_COLS = 2 * FFN_COLS + MIX_COLS
NEG = -30000.0
import os as _os
_PCS = [int(v) for v in _os.environ.get('PCS', '0,1,2,3,4,5,6').split(',')]


class Buf:
    __slots__ = ("name", "t", "last_w", "readers", "dma_sem", "dma_cnt", "free")

    def __init__(self, name, t):
        self.name = name
        self.t = t
        self.last_w = None
        self.readers = {}
        self.dma_sem = None
        self.dma_cnt = 0
        self.free = True

    def __getitem__(self, idx):
        return self.t[idx]


class Ring:
    def __init__(self, bufs):
        self.bufs = bufs
        self.i = 0

    def get(self):
        b = self.bufs[self.i % len(self.bufs)]
        assert b.free, f"ring buffer {b.name} still live"
        b.free = False
        self.i += 1
        return b

    @staticmethod
    def rel(*bs):
        for b in bs:
            b.free = True


class Prog:
    ENGS = ("pe", "act", "dve", "pool", "sp")

    def __init__(self, nc, stack):
        self.nc = nc
        self.stack = stack
        self.items = {e: [] for e in self.ENGS}
        self.count = {e: 0 for e in self.ENGS}
        self.pending = {e: False for e in self.ENGS}
        self.seen = {e: {} for e in self.ENGS}
        self.hist = {}
        self.sems = {}
        for e in self.ENGS:
            self.sems[e] = stack.enter_context(nc.semaphore("s_" + e))

    def sbuf(self, name, shape, dtype):
        return Buf(name, self.stack.enter_context(self.nc.sbuf_tensor(name, list(shape), dtype)))

    def psum(self, name, shape, dtype=F32):
        return Buf(name, self.stack.enter_context(self.nc.psum_tensor(name, list(shape), dtype)))

    def _dma_sem(self, b):
        if b.dma_sem is None:
            key = "d_" + b.name
            self.sems[key] = self.stack.enter_context(self.nc.semaphore(key))
            b.dma_sem = key
        return b.dma_sem

    def _waits(self, eng, reads, writes, skip_self):
        deps = []
        for b in reads:
            if b.last_w is not None:
                deps.append(b.last_w)
        for b in writes:
            if b.last_w is not None:
                deps.append(b.last_w)
            deps.extend(b.readers.values())
        seen = self.seen[eng]
        best = {}
        for k, v in deps:
            if skip_self and k == eng:
                continue
            if seen.get(k, 0) >= v:
                continue
            if best.get(k, 0) < v:
                best[k] = v
        kept = []
        for k, v in sorted(best.items(), key=lambda kv: -len(self.hist.get(kv, ()))):
            if seen.get(k, 0) >= v:
                continue
            kept.append((k, v))
            seen[k] = v
            for k2, v2 in self.hist.get((k, v), {}).items():
                if seen.get(k2, 0) < v2:
                    seen[k2] = v2
        return kept

    def op(self, eng, name, *args, reads=(), writes=(), inc=True, skip_self=False, **kw):
        fn = (name, args, kw)
        waits = self._waits(eng, reads, writes, skip_self)
        if inc:
            self.count[eng] += 1
            ev = (eng, self.count[eng])
            self.pending[eng] = False
            self.hist[ev] = dict(self.seen[eng])
        else:
            ev = (eng, self.count[eng] + 1)
            self.pending[eng] = True
        self.items[eng].append((waits, fn, (eng, 1) if inc else None))
        for b in writes:
            b.last_w = ev
            b.readers = {}
        for b in reads:
            if b not in writes:
                b.readers[eng] = ev
        return ev

    def dma(self, out, in_, reads=(), writes=(), sem_buf=None, queue="sp"):
        fn = ("dma_start", (), dict(out=out, in_=in_))
        sb = sem_buf or (writes[0] if writes else reads[0])
        key = self._dma_sem(sb)
        waits = self._waits(queue, reads, writes, False)
        sb.dma_cnt += 16
        ev = (key, sb.dma_cnt)
        self.hist[ev] = dict(self.seen[queue])
        self.items[queue].append((waits, fn, (key, 16)))
        for b in writes:
            b.last_w = ev
            b.readers = {}
        for b in reads:
            if b not in writes:
                b.readers["q_" + key] = ev
        return ev

    def wait_event(self, eng, ev):
        k, v = ev
        if self.seen[eng].get(k, 0) < v:
            self.seen[eng][k] = v
            self.items[eng].append(([(k, v)], None, None))

    def emit(self):
        nc = self.nc
        for e in self.ENGS:
            assert not self.pending[e], f"engine {e} has pending un-inc'd instrs"
        with nc.Block() as block:
            def run(engname):
                def body(engine):
                    for waits, fn, inc in self.items[engname]:
                        fuse = (fn is not None and waits and "accum_out" not in fn[2])
                        for k, v in (waits[:-1] if fuse else waits):
                            engine.wait_ge(self.sems[k], v)
                        if fn is None:
                            continue
                        r = getattr(engine, fn[0])(*fn[1], **fn[2])
                        if fuse:
                            r = r._wait_ge(self.sems[waits[-1][0]], waits[-1][1])
                        if inc is not None:
                            r.then_inc(self.sems[inc[0]], inc[1])
                return body
            block.sync(run("sp"))
            block.tensor(run("pe"))
            block.scalar(run("act"))
            block.vector(run("dve"))
            block.gpsimd(run("pool"))


def _ffn_units(wg, wu, wd):
    g = wg.reshape(KC, 128, FC, 128)
    u = wu.reshape(KC, 128, FC, 128)
    d = wd.reshape(FC, 128, KC, 128)
    parts = []
    for hf in range(2):
        for f in range(FH):
            fg = hf * FH + f
            gu = np.stack([g[:, :, fg, :], u[:, :, fg, :]], axis=0)
            parts.append(gu.transpose(2, 0, 1, 3).reshape(128, W1))
        for dc in range(KC):
            blk = d[hf * FH:(hf + 1) * FH, :, dc, :]
            parts.append(blk.transpose(1, 0, 2).reshape(128, W2))
    return np.concatenate(parts, axis=1)


def _fm_unit(ca, cb):
    a = np.stack([ca.reshape(KC, 128, 128), cb.reshape(KC, 128, 128)], axis=0)
    return a.transpose(2, 0, 1, 3).reshape(128, UW)


def _tm_unit(pieces4, half):
    a = np.concatenate(pieces4, axis=1).reshape(KC, 128, 512)[4 * half:4 * half + 4]
    return a.transpose(1, 0, 2).reshape(128, UW)


def _mixer_units(w_in, w_out):
    z = np.zeros((D_MODEL, 128), np.float32)
    aq, ak, av = w_in[:, 0:512], w_in[:, 512:640], w_in[:, 640:768]
    su, sv = w_in[:, 768:1024], w_in[:, 1024:1280]
    cq, ck, cv, cg = w_in[:, 1280:1536], w_in[:, 1536:1792], w_in[:, 1792:2048], w_in[:, 2048:2304]
    clr = z.copy()
    clr[:, 0:16] = w_in[:, 2304:2320]
    qc = [np.concatenate([aq[:, c * 64:(c + 1) * 64], aq[:, (4 + c) * 64:(5 + c) * 64]], axis=1) for c in range(4)]
    fm = [clr, ak, qc[0], qc[1], qc[2], qc[3], su[:, 0:128], su[:, 128:256],
          cq[:, 0:128], cq[:, 128:256], ck[:, 0:128], ck[:, 128:256], cg[:, 0:128], cg[:, 128:256]]
    tm = [av, sv[:, 0:128], sv[:, 128:256], ck[:, 0:128], ck[:, 128:256], cv[:, 0:128], cv[:, 128:256], z]
    perm = []
    for c in range(4):
        perm += list(range(c * 64, (c + 1) * 64)) + list(range((4 + c) * 64, (5 + c) * 64))
    perm += list(range(512, 1024))
    wo = w_out[np.array(perm), :]
    parts = [_fm_unit(fm[2 * u], fm[2 * u + 1]) for u in range(N_FMU)]
    parts += [_tm_unit(tm[4 * (u // 2):4 * (u // 2) + 4], u % 2) for u in range(N_TMU)]
    parts += [_fm_unit(wo[:, (2 * u) * 128:(2 * u + 1) * 128], wo[:, (2 * u + 1) * 128:(2 * u + 2) * 128]) for u in range(N_OUTU)]
    return np.concatenate(parts, axis=1)


def _pack_layer_weights(inp, l):
    cols = [_ffn_units(inp["ffn1_w_gate"][l], inp["ffn1_w_up"][l], inp["ffn1_w_down"][l]),
            _mixer_units(inp["w_in"][l], inp["w_out"][l]),
            _ffn_units(inp["ffn2_w_gate"][l], inp["ffn2_w_up"][l], inp["ffn2_w_down"][l])]
    return np.concatenate(cols, axis=1)


def _colT(v):
    return np.ascontiguousarray(v.reshape(-1, 128).T)


CB_ONES, CB_BONES, CB_ID, CB_OP0, CB_OP1 = 0, 128, 256, 384, 512
CB_MCUR, CB_MPRV, CB_GM, CB_TRIL = 640, 1152, 1664, 2176
CB_END = 2688
CF_UC, CF_MGT = 0, 128
CF_END = 256


def _consts():
    cb = np.zeros((128, CB_END), np.float32)
    i = np.arange(128)
    k, q = i[:, None], i[None, :]
    cb[:, CB_ONES:CB_ONES + 128] = 1.0
    cb[:, CB_BONES:CB_BONES + 128] = (k // 64 == q // 64)
    cb[:, CB_ID:CB_ID + 128] = (k == q)
    cb[:, CB_OP0:CB_OP0 + 64] = 1.0
    cb[:, CB_OP1 + 64:CB_OP1 + 128] = 1.0
    mcur = np.where(q >= k, 0.0, NEG)
    mprv = np.where(k > q, 0.0, NEG)
    same = (k // 64 == q // 64)
    gm = ((k <= q) & same).astype(np.float32)
    tril = (k <= q).astype(np.float32)
    cb[:, CB_MCUR:CB_MCUR + 512] = np.tile(mcur, (1, 4))
    cb[:, CB_MPRV:CB_MPRV + 512] = np.tile(mprv, (1, 4))
    cb[:, CB_GM:CB_GM + 512] = np.tile(gm, (1, 4))
    cb[:, CB_TRIL:CB_TRIL + 512] = np.tile(tril, (1, 4))
    cf = np.zeros((128, CF_END), np.float32)
    cf[:, CF_UC:CF_UC + 128] = gm
    cf[:, CF_MGT:CF_MGT + 128] = ((k > q) & same)
    return cb, cf


LP_G1, LP_GM, LP_G2 = 0, 8, 16
LP_GQ, LP_GK, LP_GO = 24, 25, 26
LP_SINK = 27
LP_VN = 31
LP_BS = LP_VN + 256
LP_BG = LP_BS + 256
LP_WS = LP_BG + 256
LP_WGU = LP_WS + 512
LP_END = LP_WGU + 256


def _layer_params(inp, l):
    p = np.zeros((128, LP_END), np.float32)
    p[:, LP_G1:LP_G1 + 8] = _colT(inp["ffn1_norm"][l])
    p[:, LP_GM:LP_GM + 8] = _colT(inp["mix_norm"][l])
    p[:, LP_G2:LP_G2 + 8] = _colT(inp["ffn2_norm"][l])
    p[:, LP_GQ] = np.tile(inp["attn_q_norm"][l], 2)
    p[:, LP_GK] = np.tile(inp["attn_k_norm"][l], 2)
    p[:, LP_GO] = np.tile(inp["gla_out_norm"][l], 2)
    sk = inp["attn_sinks"][l]
    p[0:64, LP_SINK:LP_SINK + 4] = sk[None, 0:4]
    p[64:128, LP_SINK:LP_SINK + 4] = sk[None, 4:8]
    p[:, LP_VN:LP_VN + 256] = inp["sgu_v_norm"][l][None, :]
    bs = inp["sgu_b"][l]
    for j in range(2):
        for gg in range(2):
            p[gg * 64:(gg + 1) * 64, LP_BS + j * 128:LP_BS + (j + 1) * 128] = bs[2 * j + gg][None, :]
    p[:, LP_BG:LP_BG + 256] = inp["gla_b_gate"][l][None, :]
    ws = inp["sgu_w"][l]
    for g in range(4):
        p[:, LP_WS + g * 128:LP_WS + (g + 1) * 128] = ws[g].T
    p[0:16, LP_WGU:LP_WGU + 256] = inp["gla_w_gate_up"][l]
    return p


def build_program(T, TT=1024, depth=DEPTH, stages=("ffn1", "mix", "A", "B", "C", "ffn2"), dbg=99, dbg2=99):
    assert T % TT == 0 and TT % 512 == 0
    NT = T // TT
    NS = TT // 512
    nc = bass.Bass("TRN2", target_bir_lowering=False)
    xT = nc.dram_tensor("xT", [D_MODEL, T], F32, kind="ExternalInput").ap()
    wts = nc.dram_tensor("wts", [128, depth * LAYER_COLS], F32, kind="ExternalInput").ap()
    lpar = nc.dram_tensor("lpar", [128, depth * LP_END], F32, kind="ExternalInput").ap()
    cstb = nc.dram_tensor("cstb", [128, CB_END], F32, kind="ExternalInput").ap()
    cstf = nc.dram_tensor("cstf", [128, CF_END], F32, kind="ExternalInput").ap()
    yT = nc.dram_tensor("yT", [D_MODEL, T], F32, kind="ExternalOutput").ap()
    wsc = nc.dram_tensor("wsc", [128, depth * LAYER_COLS], BF16, kind="Internal").ap()

    with ExitStack() as st:
        P = Prog(nc, st)

        def tens(name, shape, dt):
            return st.enter_context(nc.sbuf_tensor(name, list(shape), dt))

        xres_t = tens("xres", [128, KC, TT], F32)
        xres = [[Buf(f"xres{k}_{s}", xres_t) for s in range(NS)] for k in range(KC)]
        hT_t = tens("hT", [128, KC, TT], BF16)
        hT = [[Buf(f"hT{k}_{s}", hT_t) for s in range(NS)] for k in range(KC)]
        act_raw = tens("act_raw", [128, max(FH * TT // 2, 5120)], F32)
        act_t = act_raw[:, 0:FH * TT // 2].bitcast(BF16).rearrange("p (f t) -> p f t", f=FH)
        actT = [[Buf(f"act{f}_{s}", act_t) for s in range(NS)] for f in range(FH)]
        act_all = [b for row in actT for b in row]

        NSTG, NWB = 3, 4
        stg = [P.sbuf(f"stg{i}", [128, UW], F32) for i in range(NSTG)]
        wbf_t = [tens(f"wbf{i}", [128, UW], BF16) for i in range(NWB)]
        wbf = [[Buf(f"wbf{i}_{j}", wbf_t[i]) for j in range(3)] for i in range(NWB)]
        late_t = list(wbf_t) + [stg[i].t[:, h * (UW // 2):(h + 1) * (UW // 2)].bitcast(BF16) for i in range(NSTG) for h in range(2)]
        late_b = list(wbf) + [[Buf(f"lw{i}_{h}", None)] for i in range(NSTG) for h in range(2)]
        sq_ring = Ring([P.sbuf(f"sq{i}", [128, 512], BF16) for i in range(4)])
        f32_ring = Ring([P.sbuf(f"fs{i}", [128, 512], F32) for i in range(4)])
        ps_ring = Ring([P.psum(f"ps{i}", [128, 512]) for i in range(6)])
        gla_o = [P.psum(f"glao{i}", [128, 512]) for i in range(2)]

        cb = P.sbuf("cb", [128, CB_END], BF16)
        cf = P.sbuf("cf", [128, CF_END], F32)
        lp = P.sbuf("lp", [128, depth * LP_END], F32)
        P.dma(cf[:], cstf[:, :], writes=[cf])
        P.dma(lp[:], lpar[:, :], writes=[lp])
        for i, c0 in enumerate(range(0, CB_END, UW)):
            w = min(UW, CB_END - c0)
            P.dma(stg[i][:, 0:w], cstb[:, c0:c0 + w], writes=[stg[i]])
            P.op("dve", "tensor_copy", out=cb[:, c0:c0 + w], in_=stg[i][:, 0:w], reads=[stg[i]], writes=[cb])
        ones_b = cb.t[:, CB_ONES:CB_ONES + 128]
        bones_b = cb.t[:, CB_BONES:CB_BONES + 128]
        ident_b = cb.t[:, CB_ID:CB_ID + 128]
        opad_b = [cb.t[:, CB_OP0:CB_OP0 + 128], cb.t[:, CB_OP1:CB_OP1 + 128]]
        mcur_b = cb.t[:, CB_MCUR:CB_MCUR + 512]
        mprv_b = cb.t[:, CB_MPRV:CB_MPRV + 512]
        gm_b = cb.t[:, CB_GM:CB_GM + 512]
        tril_b = cb.t[:, CB_TRIL:CB_TRIL + 512]
        uc_f = cf.t[:, CF_UC:CF_UC + 128]
        mgt_f = cf.t[:, CF_MGT:CF_MGT + 128]

        MIX = "mix" in stages
        if MIX:
            qhat = P.sbuf("qhat", [128, 4, 512], BF16)
            khat = [P.sbuf(f"khat{l}", [128, 2, 640], BF16) for l in range(depth)]
            Vp = [P.sbuf(f"Vp{l}", [128, 5, 2, 128], BF16) for l in range(depth)]
            uT = P.sbuf("uT", [128, 2, 512], BF16)
            vhp_t = tens("vhp", [128, 4, 4, 128], BF16)
            vhp = [Buf(f"vhp{b}", vhp_t) for b in range(4)]
            lrT = P.sbuf("lrT", [16, 512], BF16)
            qd = P.sbuf("qd", [128, 2, 512], BF16)
            ki = P.sbuf("ki", [128, 2, 512], BF16)
            kstp_t = tens("kstp", [128, 4, 4, 128], BF16)
            kstp = [Buf(f"kstp{b}", kstp_t) for b in range(4)]
            Vgp_t = tens("Vgp", [128, 4, 4, 128], BF16)
            Vgp = [Buf(f"Vgp{b}", Vgp_t) for b in range(4)]
            mix_t = tens("mixedT", [128, 8, 512], BF16)
            mixA, mixB, mixC = Buf("mixA", mix_t), Buf("mixB", mix_t), Buf("mixC", mix_t)
            pT_ring = Ring([P.sbuf(f"pT{i}", [128, 512], BF16) for i in range(6)])
            aT_ring = Ring([P.sbuf(f"aT{i}", [128, 512], BF16) for i in range(4)])
            small = P.sbuf("small", [128, 16], F32)
            Sst = [[[P.sbuf(f"S{l}_{hp}_{i}", [128, 128], F32) for i in range(2)] for hp in range(2)] for l in range(depth)]
            s_cur = [[0, 0] for _ in range(depth)]
            vn_t = tens("vn", [128, 8, 4], F32)
            vn = [Buf(f"vn{i}", vn_t) for i in range(8)]
            vn_i = [0]
            mhalf = P.sbuf("mhalf", [128, 2], F32)
            Sbw = [[P.sbuf(f"Sbw{hp}_{i}", [128, 128], BF16) for i in range(9)] for hp in range(2)]
            esink = [P.sbuf(f"esink{l}", [128, 512], F32) for l in range(depth)]
            wcT = [P.sbuf(f"wcT{l}", [128, 512], BF16) for l in range(depth)]
            wgu = [P.sbuf(f"wgu{l}", [16, 256], BF16) for l in range(depth)]
            o = 0
            sp_t = act_raw[:, o:o + 1024].rearrange("p (b c) -> p b c", b=4); o += 1024
            eneg_t = act_raw[:, o:o + 1024].rearrange("p (j t) -> p j t", j=2); o += 1024
            epos_t = act_raw[:, o:o + 1024].rearrange("p (j t) -> p j t", j=2); o += 1024
            etm_t = act_raw[:, o:o + 1024].rearrange("p (b c) -> p b c", b=4); o += 1024
            sgate_t = act_raw[:, o:o + 1024].rearrange("p (j t) -> p j t", j=2); o += 1024
            spB = [Buf(f"sp{b}", sp_t) for b in range(4)]
            enegB, eposB = Buf("eneg", eneg_t), Buf("epos", epos_t)
            etmB = [Buf(f"etm{b}", etm_t) for b in range(4)]
            sgateB = Buf("sgate", sgate_t)
            alias_bufs = spB + [enegB, eposB] + etmB + [sgateB]

            for t_ in (vhp_t, kstp_t, Vgp_t):
                P.op("pool", "memset", t_[:], 0.0)
            P.op("pool", "memset", mhalf[:], -0.5, writes=[mhalf])
            for l in range(depth):
                P.op("pool", "memset", khat[l][:], 0.0, writes=[khat[l]])
                P.op("pool", "memset", Vp[l][:], 0.0, writes=[Vp[l]])
                for hp in range(2):
                    P.op("pool", "memset", Sst[l][hp][0][:], 0.0, writes=[Sst[l][hp][0]])
                lo = l * LP_END
                P.op("act", "activation", out=small[:, 0:4], in_=lp[:, lo + LP_SINK:lo + LP_SINK + 4], func=AF.Exp,
                     reads=[lp], writes=[small])
                for c in range(4):
                    P.op("dve", "tensor_scalar", out=esink[l][:, c * 128:(c + 1) * 128], in0=uc_f, scalar1=0.0,
                         scalar2=small[:, c:c + 1], op0=ALU.mult, op1=ALU.add, reads=[cf, small], writes=[esink[l]])
                P.op("dve", "tensor_tensor", out=wcT[l][:], in0=lp[:, lo + LP_WS:lo + LP_WS + 512], in1=tril_b, op=ALU.mult,
                     reads=[lp, cb], writes=[wcT[l]])
                P.op("dve", "tensor_copy", out=wgu[l][:], in_=lp[0:16, lo + LP_WGU:lo + LP_WGU + 256], reads=[lp], writes=[wgu[l]])
            for b in range(4):
                for B_ in (vhp[b], kstp[b], Vgp[b]):
                    B_.last_w = ("pool", P.count["pool"])

        ws = {"n": 0, "tile": 0, "pend": [], "stored": {}, "late": 0}

        class WB:
            def __init__(self, t, bufs):
                self.t, self.bufs = t, bufs

        def wflush(keep):
            while len(ws["pend"]) > keep:
                col0, width, i = ws["pend"].pop(0)
                ws["stored"][col0] = P.dma(wsc[:, col0:col0 + width], wbf_t[i % NWB][:, 0:width], reads=wbf[i % NWB],
                                           sem_buf=wbf[i % NWB][0])

        def tile_units():
            seq = []

            def ffn_u(c):
                for hf in range(2):
                    for f in range(FH):
                        seq.append((c, W1)); c += W1
                    for dc in range(KC):
                        seq.append((c, W2)); c += W2
            for l in range(depth):
                base = l * LAYER_COLS
                if "ffn1" in stages:
                    ffn_u(base)
                if "mix" in stages:
                    for s_ in range(NS):
                        for u in range(N_FMU + N_TMU + N_OUTU):
                            seq.append((base + FFN_COLS + u * UW, UW))
                if "ffn2" in stages:
                    ffn_u(base + FFN_COLS + MIX_COLS)
            return seq

        useq = tile_units()
        LA = 1
        ws["pos"] = 0
        ws["issued"] = 0
        ws["slot"] = {}

        def wissue(p):
            col0, width = useq[p]
            i = ws["n"]
            ws["n"] += 1
            wt, wb = wbf_t[i % NWB], wbf[i % NWB]
            sg = stg[i % NSTG]
            P.dma(sg[:, 0:width], wts[:, col0:col0 + width], writes=[sg])
            c1, c2 = (width // 4) // 64 * 64, (5 * width // 8) // 64 * 64
            P.op("pool", "tensor_copy", out=wt[:, 0:c1], in_=sg[:, 0:c1], reads=[sg], writes=[wb[0]])
            P.op("act", "activation", out=wt[:, c1:c2], in_=sg[:, c1:c2], func=AF.Copy, reads=[sg], writes=[wb[1]])
            P.op("dve", "tensor_copy", out=wt[:, c2:width], in_=sg[:, c2:width], reads=[sg], writes=[wb[2]])
            ws["pend"].append((col0, width, i))
            wflush(2)
            ws["slot"][p] = WB(wt, wb)

        def wnext(col0, width):
            if ws["tile"] == 0:
                p = ws["pos"]
                assert useq[p] == (col0, width), (p, useq[p], col0, width)
                while ws["issued"] < min(p + 1 + LA, len(useq)):
                    wissue(ws["issued"])
                    ws["issued"] += 1
                ws["pos"] += 1
                return ws["slot"].pop(p)
            j = ws["late"] % len(late_t)
            ws["late"] += 1
            wt, wb = late_t[j], late_b[j]
            P.wait_event("sp", ws["stored"][col0])
            P.dma(wt[:, 0:width], wsc[:, col0:col0 + width], writes=wb, sem_buf=wb[-1])
            return WB(wt, wb)

        def mm(ps_ap, lhsT, rhs, first, last, reads, writes, inc_all=False):
            P.op("pe", "matmul", ps_ap, lhsT=lhsT, rhs=rhs, start=first, stop=last,
                 reads=reads, writes=writes, inc=(last or inc_all), skip_self=True)

        def alias(dst, src):
            merged = {}
            for b in src:
                evs = list(b.readers.items())
                if b.last_w is not None:
                    evs.append(("w_" + b.last_w[0], b.last_w))
                for k, ev in evs:
                    if k not in merged or merged[k][1] < ev[1] or merged[k][0] != ev[0]:
                        if k in merged and merged[k][0] != ev[0]:
                            k = k + "_" + ev[0]
                        if k not in merged or merged[k][1] < ev[1]:
                            merged[k] = ev
            for b in dst:
                for k, ev in merged.items():
                    if k not in b.readers or b.readers[k][1] < ev[1]:
                        b.readers[k] = ev

        def rstd_from(pss, width, dim):
            rs = f32_ring.get()
            P.op("act", "activation", out=rs[:, 0:width], in_=pss[:, 0:width], func=AF.Ln, bias=RMS_EPS, scale=1.0 / dim,
                 reads=[pss], writes=[rs])
            P.op("act", "activation", out=rs[:, 0:width], in_=rs[:, 0:width], func=AF.Exp, scale=-0.5, reads=[rs], writes=[rs])
            return rs

        def norm_gen(lcol, s):
            tsl = slice(s * 512, (s + 1) * 512)
            pss = ps_ring.get()
            sqs = {}

            def square(k):
                sqs[k] = sq_ring.get()
                P.op("act", "activation", out=sqs[k][:], in_=xres_t[:, k, tsl], func=AF.Square, reads=[xres[k][s]], writes=[sqs[k]])

            def accum(k):
                mm(pss[:], ones_b, sqs[k][:], k == 0, k == KC - 1, [sqs[k], cb], [pss], inc_all=True)
                Ring.rel(sqs[k])

            for k in range(4):
                square(k)
            yield
            for k in range(4):
                accum(k)
            for k in range(4, 8):
                square(k)
            yield
            for k in range(4, 8):
                accum(k)
            rs = rstd_from(pss, 512, D_MODEL)
            Ring.rel(pss)
            for k in range(KC):
                P.op("dve", "scalar_tensor_tensor", out=hT_t[:, k, tsl], in0=xres_t[:, k, tsl],
                     scalar=lp[:, lcol + k:lcol + k + 1], in1=rs[:], op0=ALU.mult, op1=ALU.mult,
                     reads=[xres[k][s], rs, lp], writes=[hT[k][s]])
            Ring.rel(rs)

        def norm_to_h(lcol, only=None):
            for s in (range(NS) if only is None else [only]):
                for _ in norm_gen(lcol, s):
                    pass

        def ffn(wcol, lcol, normed=False, after_s=None):
            if not normed:
                norm_to_h(lcol)
            normed_late = False
            c = wcol
            for hf in range(2):
                def gate_up(wb, f, s):
                    wv = wb.t[:, 0:W1].rearrange("p (m k c) -> p m k c", m=2, k=KC)
                    tsl = slice(s * 512, (s + 1) * 512)
                    pg = ps_ring.get()
                    pu = ps_ring.get()
                    for m, pp in ((0, pg), (1, pu)):
                        for k in range(KC):
                            mm(pp[:], wv[:, m, k, :], hT_t[:, k, tsl], k == 0, k == KC - 1, wb.bufs + [hT[k][s]], [pp])
                    sg = f32_ring.get()
                    P.op("act", "activation", out=sg[:], in_=pg[:], func=AF.Silu, reads=[pg], writes=[sg])
                    P.op("dve", "tensor_tensor", out=act_t[:, f, tsl], in0=pu[:], in1=sg[:], op=ALU.mult,
                         reads=[pu, sg], writes=[actT[f][s]])
                    Ring.rel(pg, pu, sg)

                NHEAD = 3 if (hf == 0 and NS > 1 and not normed_late) else 0
                head = []
                for f in range(NHEAD):
                    head.append(wnext(c, W1))
                    c += W1
                for s in range(NS):
                    for f in range(NHEAD):
                        gate_up(head[f], f, s)
                for f in range(NHEAD, FH):
                    wb = wnext(c, W1)
                    c += W1
                    for s in range(NS):
                        gate_up(wb, f, s)
                def down(wb, dc, s):
                    wv = wb.t[:, 0:W2].rearrange("p (k c) -> p k c", k=FH)
                    tsl = slice(s * 512, (s + 1) * 512)
                    py = ps_ring.get()
                    for f in range(FH):
                        mm(py[:], wv[:, f, :], act_t[:, f, tsl], f == 0, f == FH - 1, wb.bufs + [actT[f][s]], [py])
                    P.op("dve", "scalar_tensor_tensor", out=xres_t[:, dc, tsl], in0=py[:], scalar=0.5,
                         in1=xres_t[:, dc, tsl], op0=ALU.mult, op1=ALU.add,
                         reads=[py, xres[dc][s]], writes=[xres[dc][s]])
                    Ring.rel(py)

                if hf == 1 and ws["tile"] > 0 and NS > 1:
                    units = []
                    for dc in range(KC):
                        units.append(wnext(c, W2))
                        c += W2
                    gen = None
                    for s in range(NS):
                        for dc in range(KC):
                            down(units[dc], dc, s)
                            if gen is not None and dc >= 1:
                                next(gen, None)
                        if gen is not None:
                            for _ in gen:
                                pass
                        gen = after_s(s) if after_s is not None else None
                    if gen is not None:
                        for _ in gen:
                            pass
                else:
                    for dc in range(KC):
                        wb = wnext(c, W2)
                        c += W2
                        for s in range(NS):
                            down(wb, dc, s)
                    if hf == 1 and after_s is not None:
                        for s in range(NS):
                            for _ in after_s(s):
                                pass

        def mixer_sub(wcol, l, s, first):
            lo = l * LP_END
            t0 = s * 512
            tsl = slice(t0, t0 + 512)
            c = wcol
            kh, vp = khat[l], Vp[l]
            doA, doB, doC = "A" in stages, "B" in stages, "C" in stages
            if not first:
                P.op("pool", "tensor_copy", out=kh[:, :, 0:128], in_=kh[:, :, 512:640], reads=[kh], writes=[kh])
                P.op("pool", "tensor_copy", out=vp[:, 0, :, :], in_=vp[:, 4, :, :], reads=[vp], writes=[vp])

            def diag2(t3, j):
                flat = t3.rearrange("p h c -> p (h c)")
                if j == 0:
                    return flat[:, 0:384].rearrange("p (a c) -> p a c", c=192)[:, :, 0:64]
                return flat[:, 128:512].rearrange("p (a c) -> p a c", c=192)[:, :, 128:192]

            def proj_fm(wb, j, M=128):
                wv = wb.t[:, 0:UW].rearrange("p (j k c) -> p j k c", j=2, k=KC)
                ps = ps_ring.get()
                for k in range(KC):
                    mm(ps[0:M, :], wv[:, j, k, 0:M], hT_t[:, k, tsl], k == 0, k == KC - 1, wb.bufs + [hT[k][s]], [ps])
                return ps

            def qk_part1(ps):
                sq = sq_ring.get()
                P.op("act", "activation", out=sq[:], in_=ps[:], func=AF.Square, reads=[ps], writes=[sq])
                return sq

            def qk_part2(ps, sq, is_k, cq):
                ps2 = ps_ring.get()
                mm(ps2[:], bones_b, sq[:], True, True, [sq, cb], [ps2])
                Ring.rel(sq)
                rs = rstd_from(ps2, 512, 64)
                Ring.rel(ps2)
                if is_k:
                    for g in range(2):
                        r = slice(g * 64, (g + 1) * 64)
                        P.op("dve", "scalar_tensor_tensor", out=kh[r, g, 128:640], in0=ps[r, :],
                             scalar=lp[r, lo + LP_GK:lo + LP_GK + 1], in1=rs[r, :], op0=ALU.mult, op1=ALU.mult,
                             reads=[ps, rs, lp], writes=[kh])
                else:
                    P.op("dve", "scalar_tensor_tensor", out=qhat[:, cq, :], in0=ps[:],
                         scalar=lp[:, lo + LP_GQ:lo + LP_GQ + 1], in1=rs[:], op0=ALU.mult, op1=ALU.mult,
                         reads=[ps, rs, lp], writes=[qhat])
                Ring.rel(ps, rs)

            def qk_norm(ps, is_k, cq):
                qk_part2(ps, qk_part1(ps), is_k, cq)

            if dbg < 1:
                return
            wb = wnext(c, UW); c += UW
            ps = proj_fm(wb, 0, M=16)
            P.op("act", "activation", out=lrT[:, :], in_=ps[0:16, :], func=AF.Copy, reads=[ps], writes=[lrT])
            Ring.rel(ps)
            ps = proj_fm(wb, 1)
            if doA:
                qk_norm(ps, True, 0)
            else:
                Ring.rel(ps)
            if dbg < 2:
                return
            if doC:
                pls, zs = [], []
                for b in range(4):
                    pl = ps_ring.get()
                    mm(pl[:, 0:256], lrT[0:16, b * 128:(b + 1) * 128], wgu[l][:, :], True, True, [lrT, wgu[l]], [pl])
                    pls.append(pl)
                for b in range(4):
                    z = f32_ring.get()
                    P.op("dve", "tensor_tensor", out=z[:, 0:256], in0=pls[b][:, 0:256], in1=lp[:, lo + LP_BG:lo + LP_BG + 256],
                         op=ALU.add, reads=[pls[b], lp], writes=[z])
                    Ring.rel(pls[b])
                    zs.append(z)
                for b in range(4):
                    P.op("act", "activation", out=zs[b][:, 0:256], in_=zs[b][:, 0:256], func=AF.Exp, scale=-1.0, reads=[zs[b]], writes=[zs[b]])
                for b in range(4):
                    P.op("act", "activation", out=sp_t[:, b, :], in_=zs[b][:, 0:256], func=AF.Ln, bias=1.0, reads=[zs[b]], writes=[spB[b]])
                    Ring.rel(zs[b])
            qpend = [None]
            for u in range(1, N_FMU):
                if u == 3:
                    if qpend[0] is not None:
                        qk_part2(*qpend[0])
                        qpend[0] = None
                    if doC:
                        pcs = []
                        for b in range(4):
                            pc = ps_ring.get()
                            for j in range(2):
                                mm(pc[:, j * 128:(j + 1) * 128], sp_t[:, b, j * 128:(j + 1) * 128], uc_f, True, True, [spB[b], cf], [pc], inc_all=True)
                            mm(pc[:, 256:512], mgt_f, sp_t[:, b, :], True, True, [spB[b], cf], [pc], inc_all=True)
                            pcs.append(pc)
                        for b in range(4):
                            bsl = slice(b * 128, (b + 1) * 128)
                            pcv = pcs[b].t[:, 0:256].rearrange("p (j t) -> p j t", j=2)
                            P.op("act", "activation", out=eneg_t[:, :, bsl], in_=pcv, func=AF.Exp, scale=-1.0 / 16, reads=[pcs[b]], writes=[enegB])
                            P.op("act", "activation", out=epos_t[:, :, bsl], in_=pcv, func=AF.Exp, scale=1.0 / 16, reads=[pcs[b]], writes=[eposB])
                            P.op("act", "activation", out=etm_t[:, b, :], in_=pcs[b][:, 256:512], func=AF.Exp, scale=-1.0 / 16, reads=[pcs[b]], writes=[etmB[b]])
                            Ring.rel(pcs[b])
                wb = wnext(c, UW); c += UW
                for j in range(2):
                    ci = 2 * u + j
                    need = (ci <= 5 and doA) or (ci in (6, 7) and doB) or (ci >= 8 and doC)
                    if not need:
                        continue
                    ps = proj_fm(wb, j)
                    if ci <= 5:
                        cur = (ps, qk_part1(ps), False, ci - 2)
                        if qpend[0] is not None:
                            qk_part2(*qpend[0])
                        qpend[0] = cur
                        continue
                    if qpend[0] is not None:
                        qk_part2(*qpend[0])
                        qpend[0] = None
                    if ci <= 7:
                        P.op("act", "activation", out=uT[:, ci - 6, :], in_=ps[:], func=AF.Gelu_apprx_tanh, reads=[ps], writes=[uT])
                        Ring.rel(ps)
                    elif ci <= 9:
                        P.op("dve", "scalar_tensor_tensor", out=qd[:, ci - 8, :], in0=ps[:], scalar=0.125, in1=eneg_t[:, ci - 8, :],
                             op0=ALU.mult, op1=ALU.mult, reads=[ps, enegB], writes=[qd])
                        Ring.rel(ps)
                    elif ci <= 11:
                        P.op("dve", "tensor_tensor", out=ki[:, ci - 10, :], in0=ps[:], in1=epos_t[:, ci - 10, :], op=ALU.mult,
                             reads=[ps, eposB], writes=[ki])
                        Ring.rel(ps)
                    else:
                        P.op("act", "activation", out=sgate_t[:, ci - 12, :], in_=ps[:], func=AF.Silu, reads=[ps], writes=[sgateB])
                        Ring.rel(ps)
            if qpend[0] is not None:
                qk_part2(*qpend[0])
                qpend[0] = None
            if dbg < 3:
                return
            for u in range(N_TMU // 2):
                wbs = [wnext(c, UW), wnext(c + UW, UW)]
                c += 2 * UW
                wvs = [w_.t[:, 0:UW].rearrange("p (k c) -> p k c", k=4) for w_ in wbs]
                for b in range(4):
                    pieces = [pi for pi in range(4 * u, 4 * u + 4) if pi < 7 and pi in _PCS and ((pi == 0 and doA) or (pi in (1, 2) and doB) or (pi >= 3 and doC))]
                    if not pieces:
                        continue
                    ps = ps_ring.get()
                    for k in range(KC):
                        mm(ps[:, :], hT_t[:, k, t0 + b * 128:t0 + (b + 1) * 128], wvs[k // 4][:, k % 4, :], k == 0, k == KC - 1,
                           wbs[k // 4].bufs + [hT[k][s]], [ps])
                    pieces = sorted(pieces, key=lambda q_: 0 if q_ >= 3 else 1)
                    dve_first = [kstp[b]] if any(q_ >= 3 for q_ in pieces) else []
                    for pi in pieces:
                        co = (pi % 4) * 128
                        if pi == 0:
                            vflat = vp.t[:, :, :, :].rearrange("p b g c -> p (b g c)")
                            vdst = vflat[:, (1 + b) * 256 - 128:(1 + b) * 256 + 256].rearrange("p (a c) -> p a c", c=192)[:, :, 128:192]
                            P.op("act", "activation", out=vdst, in_=ps[:, co:co + 128].rearrange("p (a c) -> p a c", a=2),
                                 func=AF.Copy, reads=[ps] + dve_first, writes=[vp])
                        elif pi <= 2:
                            j = pi - 1
                            vg = f32_ring.get()
                            P.op("act", "activation", out=vg[:, 0:128], in_=ps[:, co:co + 128], func=AF.Gelu_apprx_tanh, reads=[ps] + dve_first, writes=[vg])
                            vs = vn_i[0] % 8
                            vn_i[0] += 1
                            for gg in range(2):
                                P.op("act", "activation", out=vg[:, 256 + gg * 64:256 + (gg + 1) * 64], in_=vg[:, gg * 64:(gg + 1) * 64], func=AF.Square,
                                     accum_out=vn_t[:, vs, gg:gg + 1], reads=[vg], writes=[vg, vn[vs]])
                            P.op("pool", "tensor_scalar", out=vn_t[:, vs, 2:4], in0=vn_t[:, vs, 0:2], scalar1=1.0 / 64, scalar2=RMS_EPS,
                                 op0=ALU.mult, op1=ALU.add, reads=[vn[vs]], writes=[vn[vs]])
                            P.op("pool", "tensor_tensor", out=vn_t[:, vs, 2:4], in0=vn_t[:, vs, 2:4], in1=mhalf[:, :], op=ALU.pow,
                                 reads=[vn[vs], mhalf], writes=[vn[vs]])
                            for gg in range(2):
                                g4 = 2 * j + gg
                                P.op("dve", "scalar_tensor_tensor", out=vhp_t[:, b, g4, gg * 64:(gg + 1) * 64], in0=vg[:, gg * 64:(gg + 1) * 64],
                                     scalar=vn_t[:, vs, 2 + gg:3 + gg], in1=lp[:, lo + LP_VN + g4 * 64:lo + LP_VN + (g4 + 1) * 64],
                                     op0=ALU.mult, op1=ALU.mult, reads=[vg, vn[vs], lp], writes=[vhp[b]])
                            Ring.rel(vg)
                        elif pi <= 4:
                            j = pi - 3
                            P.op("dve", "tensor_tensor", out=diag2(kstp_t[:, b, :, :], j), in0=ps[:, co:co + 128].rearrange("p (a c) -> p a c", a=2),
                                 in1=etm_t[:, b, j * 128:(j + 1) * 128].rearrange("p (a c) -> p a c", a=2), op=ALU.mult,
                                 reads=[ps, etmB[b]], writes=[kstp[b]])
                        else:
                            j = pi - 5
                            P.op("dve", "tensor_copy", out=diag2(Vgp_t[:, b, :, :], j), in_=ps[:, co:co + 128].rearrange("p (a c) -> p a c", a=2),
                                 reads=[ps], writes=[Vgp[b]])
                    Ring.rel(ps)
            if dbg < 4:
                return
            gsq = [None, None]

            def gla_norm_p1():
                for hp in range(2):
                    gsq[hp] = sq_ring.get()
                    P.op("act", "activation", out=gsq[hp][:], in_=gla_o[hp][:], func=AF.Square, reads=[gla_o[hp]], writes=[gsq[hp]])

            def gla_norm_p2():
                for hp in range(2):
                    po = gla_o[hp]
                    ps2 = ps_ring.get()
                    mm(ps2[:], bones_b, gsq[hp][:], True, True, [gsq[hp], cb], [ps2])
                    Ring.rel(gsq[hp])
                    rs = rstd_from(ps2, 512, 64)
                    Ring.rel(ps2)
                    tmp = f32_ring.get()
                    P.op("dve", "scalar_tensor_tensor", out=tmp[:], in0=po[:], scalar=lp[:, lo + LP_GO:lo + LP_GO + 1], in1=rs[:],
                         op0=ALU.mult, op1=ALU.mult, reads=[po, rs, lp], writes=[tmp])
                    P.op("dve", "tensor_tensor", out=mix_t[:, 6 + hp, :], in0=tmp[:], in1=sgate_t[:, hp, :], op=ALU.mult,
                         reads=[tmp, sgateB], writes=[mixC])
                    Ring.rel(rs, tmp)

            aTs = [None] * 4
            if doC:
                for hp in range(2):
                    sc_ = Sst[l][hp][s_cur[l][hp]]
                    P.op("pool", "tensor_copy", out=Sbw[hp][0][:, :], in_=sc_[:, :], reads=[sc_], writes=[Sbw[hp][0]])
                for b in range(4):
                    bsl = slice(b * 128, (b + 1) * 128)
                    pAs = [ps_ring.get(), ps_ring.get()]
                    aT = aT_ring.get()
                    aTs[b] = aT
                    aT4 = aT.t[:, :].rearrange("p (j h t) -> p j h t", j=2, h=2)
                    for hh in range(2):
                        r = slice(hh * 64, (hh + 1) * 64)
                        for j in range(2):
                            mm(pAs[hh][:, j * 128:(j + 1) * 128], ki[r, j, bsl], qd[r, j, bsl], True, True, [ki, qd], [pAs[hh]], inc_all=True)
                        P.op("dve", "tensor_tensor", out=aT4[:, :, hh, :], in0=pAs[hh].t[:, 0:256].rearrange("p (j t) -> p j t", j=2),
                             in1=gm_b[:, 0:256].rearrange("p (j t) -> p j t", j=2), op=ALU.mult, reads=[pAs[hh], cb], writes=[aT])
                    Ring.rel(*pAs)
                    pdl = [gla_o[0], gla_o[1]] if b % 2 == 1 else [ps_ring.get(), ps_ring.get()]
                    for cc in range(2):
                        r = slice(cc * 64, (cc + 1) * 64)
                        for hp in range(2):
                            osl = slice(hp * 128, (hp + 1) * 128)
                            mm(pdl[cc][:, osl], kstp_t[r, b, 2 * hp, :], Vgp_t[r, b, 2 * hp, :], True, False, [kstp[b], Vgp[b]], [pdl[cc]], inc_all=True)
                            mm(pdl[cc][:, osl], kstp_t[r, b, 2 * hp + 1, :], Vgp_t[r, b, 2 * hp + 1, :], False, True, [kstp[b], Vgp[b]], [pdl[cc]], inc_all=True)
                    for cc in range(2):
                        tk = b * 128 + cc * 64
                        for hp in range(2):
                            so = Sst[l][hp][s_cur[l][hp]]
                            s_cur[l][hp] ^= 1
                            sn = Sst[l][hp][s_cur[l][hp]]
                            P.op("dve", "scalar_tensor_tensor", out=sn[:, :], in0=so[:, :], scalar=eneg_t[:, hp, tk + 63:tk + 64],
                                 in1=pdl[cc][:, hp * 128:(hp + 1) * 128], op0=ALU.mult, op1=ALU.add, reads=[so, enegB, pdl[cc]], writes=[sn])
                            sbn = Sbw[hp][2 * b + cc + 1]
                            P.op("pool", "tensor_copy", out=sbn[:, :], in_=sn[:, :], reads=[sn], writes=[sbn])
                    if b % 2 == 0:
                        Ring.rel(*pdl)

            for b in range(4):
                bsl = slice(b * 128, (b + 1) * 128)
                if doB:
                    pz = ps_ring.get()
                    for j in range(2):
                        for gg in range(2):
                            g4 = 2 * j + gg
                            mm(pz[:, j * 128:(j + 1) * 128], vhp_t[:, b, g4, :], wcT[l][:, g4 * 128:(g4 + 1) * 128], gg == 0, gg == 1,
                               [vhp[b], wcT[l]], [pz], inc_all=True)
                    tz = f32_ring.get()
                    P.op("dve", "tensor_tensor", out=tz[:, 0:256], in0=pz[:, 0:256], in1=lp[:, lo + LP_BS:lo + LP_BS + 256], op=ALU.add,
                         reads=[pz, lp], writes=[tz])
                    P.op("dve", "tensor_tensor", out=mix_t[:, 4:6, bsl], in0=tz.t[:, 0:256].rearrange("p (j t) -> p j t", j=2),
                         in1=uT[:, :, bsl], op=ALU.mult, reads=[tz, uT], writes=[mixB])
                    Ring.rel(pz, tz)

            def gla_out(b):
                bsl = slice(b * 128, (b + 1) * 128)
                aT = aTs[b]
                for hp in range(2):
                    po = gla_o[hp]
                    mm(po[:, bsl], Vgp_t[:, b, 2 * hp, :], aT[:, (2 * hp) * 128:(2 * hp + 1) * 128], True, False, [Vgp[b], aT], [po], inc_all=True)
                    mm(po[:, bsl], Vgp_t[:, b, 2 * hp + 1, :], aT[:, (2 * hp + 1) * 128:(2 * hp + 2) * 128], False, False, [Vgp[b], aT], [po], inc_all=True)
                    for cc in range(2):
                        tk = b * 128 + cc * 64
                        sbc = Sbw[hp][2 * b + cc]
                        mm(po[:, tk:tk + 64], sbc[:, :], qd[:, hp, tk:tk + 64], False, cc == 1, [sbc, qd], [po], inc_all=True)
                Ring.rel(aT)

            def attn_block(b):
                bsl = slice(b * 128, (b + 1) * 128)
                if True:
                    pts = []
                    for g in range(2):
                        for cur in (0, 1):
                            if first and b == 0 and not cur:
                                continue
                            pS = ps_ring.get()
                            kb = b + cur
                            mm(pS[:], kh[:, g, kb * 128:(kb + 1) * 128], qhat[:, :, bsl], True, False, [kh, qhat], [pS], inc_all=True)
                            mm(pS[:], ident_b, mcur_b if cur else mprv_b, False, True, [cb], [pS])
                            pt = pT_ring.get()
                            P.op("act", "activation", out=pt[:], in_=pS[:], func=AF.Exp, scale=0.125, reads=[pS], writes=[pt])
                            Ring.rel(pS)
                            pts.append((g, kb, pt))
                    if dbg < 6:
                        Ring.rel(*[p_[2] for p_ in pts])
                        return
                    po = ps_ring.get()
                    pd = ps_ring.get()
                    for i, (g, kb, pt) in enumerate(pts):
                        mm(pd[:], opad_b[g], pt[:], i == 0, i == len(pts) - 1, [cb, pt], [pd], inc_all=True)
                    for i, (g, kb, pt) in enumerate(pts):
                        mm(po[:], vp[:, kb, g, :], pt[:], i == 0, i == len(pts) - 1, [vp, pt], [po], inc_all=True)
                    Ring.rel(*[p_[2] for p_ in pts])
                    if dbg < 7:
                        Ring.rel(po, pd)
                        return
                    dt = f32_ring.get()
                    P.op("dve", "tensor_tensor", out=dt[:], in0=pd[:], in1=esink[l][:], op=ALU.add, reads=[pd, esink[l]], writes=[dt])
                    P.op("dve", "reciprocal", out=dt[:], in_=dt[:], reads=[dt], writes=[dt])
                    P.op("dve", "tensor_tensor", out=mix_t[:, 0:4, bsl], in0=po.t[:, :].rearrange("p (c q) -> p c q", c=4),
                         in1=dt.t[:, :].rearrange("p (c q) -> p c q", c=4), op=ALU.mult, reads=[po, dt], writes=[mixA])
                    Ring.rel(po, pd, dt)

            for b in range(4):
                if doA and b < 3:
                    attn_block(b)
                if doC:
                    gla_out(b)
            if doC:
                gla_norm_p1()
            if doA:
                attn_block(3)
            if doC:
                gla_norm_p2()
            if dbg < 8:
                return
            for u in range(N_OUTU):
                wb = wnext(c, UW); c += UW
                wv = wb.t[:, 0:UW].rearrange("p (j k c) -> p j k c", j=2, k=KC)
                for j in range(2):
                    dc = 2 * u + j
                    ks = [k for k in range(KC) if (k < 4 and doA) or (k in (4, 5) and doB) or (k >= 6 and doC)]
                    if not ks:
                        continue
                    ps = ps_ring.get()
                    for i, k in enumerate(ks):
                        mm(ps[:], wv[:, j, k, :], mix_t[:, k, :], i == 0, i == len(ks) - 1,
                           wb.bufs + [mixA if k < 4 else (mixB if k < 6 else mixC)], [ps])
                    P.op("dve", "tensor_tensor", out=xres_t[:, dc, tsl], in0=ps[:], in1=xres_t[:, dc, tsl], op=ALU.add,
                         reads=[ps, xres[dc][s]], writes=[xres[dc][s]])
                    Ring.rel(ps)

        def mixer(wcol, l, it, normed=False):
            if not normed:
                norm_to_h(l * LP_END + LP_GM)
            alias(alias_bufs, act_all)
            for s in range(NS):
                mixer_sub(wcol, l, s, first=(it == 0 and s == 0))
                if "ffn2" in stages and s < NS - 1:
                    norm_to_h(l * LP_END + LP_G2, only=s)
            if "ffn2" in stages:
                norm_to_h(l * LP_END + LP_G2, only=NS - 1)
            alias(act_all, alias_bufs)

        xT_v = xT.rearrange("(k p) t -> p k t", p=128)
        yT_v = yT.rearrange("(k p) t -> p k t", p=128)
        out_evs = []
        F1, F2 = "ffn1" in stages, "ffn2" in stages
        for it in range(NT):
            t0 = it * TT
            ws["tile"] = it
            for s_ in range(NS):
                for k in range(KC):
                    P.dma(xres_t[:, k, s_ * 512:(s_ + 1) * 512], xT_v[:, k, t0 + s_ * 512:t0 + (s_ + 1) * 512],
                          writes=[xres[k][s_]])

            def store(s_, t0=t0):
                for k in range(KC):
                    out_evs.append(P.dma(yT_v[:, k, t0 + s_ * 512:t0 + (s_ + 1) * 512], xres_t[:, k, s_ * 512:(s_ + 1) * 512],
                                         reads=[xres[k][s_]]))
                return iter(())

            stored = False
            for l in range(depth):
                base = l * LAYER_COLS
                lo_ = l * LP_END
                if F1:
                    nxt = (lambda s_, lo_=lo_: norm_gen(lo_ + LP_GM, s_)) if MIX else None
                    ffn(base, lo_ + LP_G1, normed=(l > 0 and F2), after_s=nxt)
                if MIX:
                    mixer(base + FFN_COLS, l, it, normed=F1)
                if F2:
                    if l + 1 < depth and F1:
                        nxt = (lambda s_, lo2=(l + 1) * LP_END: norm_gen(lo2 + LP_G1, s_))
                    elif l + 1 == depth:
                        nxt = store
                        stored = True
                    else:
                        nxt = None
                    ffn(base + FFN_COLS + MIX_COLS, lo_ + LP_G2, normed=MIX, after_s=nxt)
            wflush(0)
            if it == 0 and NT > 1:
                alias([b_ for bl in late_b[NWB:] for b_ in bl], stg)
            if not stored:
                for s_ in range(NS):
                    store(s_)
        for ev in out_evs:
            P.wait_event("sp", ev)
        P.emit()
    return nc


_NC_CACHE = {}


def kernel(**inp):
    inp = {k: np.asarray(v) for k, v in inp.items()}
    x = inp["x"]
    B, T, _ = x.shape
    key = (T,)
    if key not in _NC_CACHE:
        _NC_CACHE[key] = build_program(T)
    nc = _NC_CACHE[key]
    wts = np.concatenate([_pack_layer_weights(inp, l) for l in range(DEPTH)], axis=1)
    lpar = np.concatenate([_layer_params(inp, l) for l in range(DEPTH)], axis=1)
    cb, cf = _consts()
    in_maps = [{"xT": np.ascontiguousarray(x[b].T), "wts": wts, "lpar": lpar, "cstb": cb, "cstf": cf} for b in range(B)]
    res = run_bass_kernel_spmd(nc, in_maps, core_ids=list(range(B)))
    return np.stack([np.ascontiguousarray(res.results[b]["yT"].T) for b in range(B)], axis=0)
```
